# Optimizing a Trainium2 kernel written in Bass

```python
import jax
import jax.numpy as jnp
from jax import lax
import numpy as np

D_MODEL = 1024
BATCH = 16
SEQ = 2048
DEPTH = 2

HEAD_DIM = 64
N_HEADS = D_MODEL // HEAD_DIM
H_MOBA = N_HEADS // 4
H_NSA = (N_HEADS - H_MOBA) // 2
H_DIL = N_HEADS - H_MOBA - H_NSA
H_NSA_KV = 2
NSA_GROUP = H_NSA // H_NSA_KV
ROPE_DIM = HEAD_DIM // 4
ROPE_THETA = 500000.0
MOBA_BLOCK = 256
MOBA_TOPK = 3
MOBA_QCHUNK = 16
NSA_CMP_LEN = 32
NSA_CMP_STRIDE = 16
NSA_CMP_HIDDEN = 128
NSA_SEL_BLOCK = 64
NSA_N_SEL = 6
NSA_WINDOW = 512
NSA_QCHUNK = 64
DIL_CFG = ((128, 1), (512, 4), (2048, 16))
DIL_HEADS_PER_GROUP = H_DIL // len(DIL_CFG)
BAND_BLOCK = 128
D_FF = 2816
CONV_WIDTH = 3
EPS = 1e-6
NEG = -1e30
TINY = 1e-30
FORCE = 1e9
SCALE = HEAD_DIM ** -0.5
QKV_A = H_MOBA * HEAD_DIM
Q_B = H_NSA * HEAD_DIM
KV_B = H_NSA_KV * HEAD_DIM
G_B = H_NSA * 3
QKV_C = H_DIL * HEAD_DIM
IN_SPLITS = (QKV_A, QKV_A, QKV_A, Q_B, KV_B, KV_B, KV_B, KV_B, KV_B, KV_B, G_B, QKV_C, QKV_C, QKV_C)
D_IN = sum(IN_SPLITS)

kernel_name = 'hymba_style_moba_nsa_dilated_convffn'


def rms_norm(x, g):
    xf = x.astype(jnp.float32)
    y = xf * lax.rsqrt(jnp.mean(xf * xf, axis=-1, keepdims=True) + EPS)
    return (y * g.astype(jnp.float32)).astype(x.dtype)


def apply_rope(x, pos):
    half = ROPE_DIM // 2
    inv_freq = ROPE_THETA ** (-jnp.arange(half, dtype=jnp.float32) * 2.0 / ROPE_DIM)
    ang = pos.astype(jnp.float32)[..., None] * inv_freq
    c, s = jnp.cos(ang), jnp.sin(ang)
    xf = x.astype(jnp.float32)
    x1, x2, rest = xf[..., :half], xf[..., half:ROPE_DIM], xf[..., ROPE_DIM:]
    out = jnp.concatenate([x1 * c - x2 * s, x2 * c + x1 * s, rest], axis=-1)
    return out.astype(x.dtype)


def masked_softmax(s, mask):
    s = jnp.where(mask, s, NEG)
    m = jnp.max(s, axis=-1, keepdims=True)
    p = jnp.where(mask, jnp.exp(s - m), 0.0)
    den = jnp.sum(p, axis=-1, keepdims=True)
    lse = (m + jnp.log(jnp.maximum(den, TINY)))[..., 0]
    return p / jnp.maximum(den, TINY), lse


def banded_attention(q, k, v, max_dist, block):
    B, Hk, G, L, hd = q.shape
    nblk = L // block
    nprev = -(-max_dist // block)
    pad = nprev * block
    nk = (nprev + 1) * block
    def windows(t):
        tp = jnp.pad(t, ((0, 0), (0, 0), (pad, 0), (0, 0)))
        parts = [tp[:, :, p * block:p * block + L].reshape(B, Hk, nblk, block, hd) for p in range(nprev + 1)]
        return jnp.concatenate(parts, axis=3)
    kw, vw = windows(k), windows(v)
    qb = q.reshape(B, Hk, G, nblk, block, hd)
    s = jnp.einsum('bkgnqd,bkncd->bkgnqc', qb, kw).astype(jnp.float32) * SCALE
    qpos = jnp.arange(L).reshape(nblk, block)
    kpos = jnp.arange(nblk)[:, None] * block - pad + jnp.arange(nk)[None, :]
    dist = qpos[:, :, None] - kpos[:, None, :]
    mask = (dist >= 0) & (dist <= max_dist) & (kpos[:, None, :] >= 0)
    p, lse = masked_softmax(s, mask)
    o = jnp.einsum('bkgnqc,bkncd->bkgnqd', p.astype(v.dtype), vw)
    return o.reshape(B, Hk, G, L, hd), lse.reshape(B, Hk, G, L)


def moba_mixer(q, k, v):
    B, S, H, hd = q.shape
    nb = -(-S // MOBA_BLOCK)
    Sp = nb * MOBA_BLOCK
    qh = q.transpose(0, 2, 1, 3)
    def blocks(t):
        t = jnp.pad(t.transpose(0, 2, 1, 3), ((0, 0), (0, 0), (0, Sp - S), (0, 0)))
        return t.reshape(B, H, nb, MOBA_BLOCK, hd)
    kh, vh = blocks(k), blocks(v)
    tpos = jnp.arange(S)
    cur = tpos // MOBA_BLOCK
    idx_own = jnp.broadcast_to(cur, (B, H, S))[..., None]
    n_top = min(MOBA_TOPK, nb - 1)
    if n_top > 0:
        kmean = jnp.mean(kh.astype(jnp.float32), axis=3)
        gate = jnp.einsum('bhtd,bhnd->bhtn', qh.astype(jnp.float32), kmean)
        gate = jnp.where(jnp.arange(nb)[None, :] < cur[:, None], gate, NEG)
        _, idx_top = lax.top_k(gate, n_top)
        idx = jnp.concatenate([idx_top, idx_own], axis=-1)
        blk_ok = jnp.concatenate([jnp.arange(n_top)[None, :] < cur[:, None], jnp.ones((S, 1), bool)], axis=-1)
    else:
        idx = idx_own
        blk_ok = jnp.ones((S, 1), bool)
    nsel = idx.shape[-1]
    C = MOBA_QCHUNK
    nc = S // C
    qc = qh.reshape(B, H, nc, C, hd).transpose(2, 0, 1, 3, 4)
    ic = idx.reshape(B, H, nc, C, nsel).transpose(2, 0, 1, 3, 4)
    okc = blk_ok.reshape(nc, C, nsel)
    tc = tpos.reshape(nc, C)
    bi = jnp.arange(B)[:, None, None, None]
    hi = jnp.arange(H)[None, :, None, None]
    offs = jnp.arange(MOBA_BLOCK)
    def chunk(args):
        qi, ii, oki, ti = args
        kg = kh[bi, hi, ii]
        vg = vh[bi, hi, ii]
        s = jnp.einsum('bhcd,bhcnld->bhcnl', qi, kg).astype(jnp.float32) * SCALE
        kpos = ii[..., None] * MOBA_BLOCK + offs
        mask = oki[None, None, :, :, None] & (kpos <= ti[None, None, :, None, None])
        p, _ = masked_softmax(s.reshape(B, H, C, nsel * MOBA_BLOCK), mask.reshape(B, H, C, nsel * MOBA_BLOCK))
        return jnp.einsum('bhcm,bhcmd->bhcd', p.astype(vg.dtype), vg.reshape(B, H, C, nsel * MOBA_BLOCK, hd))
    o = lax.map(chunk, (qc, ic, okc, tc))
    return o.transpose(1, 0, 3, 2, 4).reshape(B, S, H * hd)


def nsa_mixer(q, k_cmp, v_cmp, k_sel, v_sel, k_win, v_win, gate_logits, kn_cmp, pe_k, pe_v, wk1, wk2, wv1, wv2):
    B, S, _, hd = q.shape
    qg = q.reshape(B, S, H_NSA_KV, NSA_GROUP, hd)
    tpos = jnp.arange(S)
    n_cmp = (S - NSA_CMP_LEN) // NSA_CMP_STRIDE + 1
    starts = np.arange(n_cmp) * NSA_CMP_STRIDE
    gidx = starts[:, None] + np.arange(NSA_CMP_LEN)[None, :]
    ends = jnp.asarray(starts + NSA_CMP_LEN - 1)
    def compress(t, pe, w1, w2):
        blk = t[:, gidx] + pe[:, None, :]
        blk = blk.transpose(0, 1, 3, 2, 4).reshape(B, n_cmp, H_NSA_KV, NSA_CMP_LEN * hd)
        return jax.nn.gelu(blk @ w1) @ w2
    kc = apply_rope(rms_norm(compress(k_cmp, pe_k, wk1, wk2), kn_cmp), ends[:, None])
    vc = compress(v_cmp, pe_v, wv1, wv2)
    s = jnp.einsum('btkgd,bnkd->bkgtn', qg, kc).astype(jnp.float32) * SCALE
    p_cmp, _ = masked_softmax(s, ends[None, :] <= tpos[:, None])
    o_cmp = jnp.einsum('bkgtn,bnkd->btkgd', p_cmp.astype(vc.dtype), vc)
    n_slc = S // NSA_SEL_BLOCK
    j = np.arange(n_slc)
    overlap = (starts[:, None] < (j[None, :] + 1) * NSA_SEL_BLOCK) & (starts[:, None] + NSA_CMP_LEN > j[None, :] * NSA_SEL_BLOCK)
    imp = jnp.einsum('bkgtn,nj->bktj', p_cmp, jnp.asarray(overlap, jnp.float32))
    cur = tpos // NSA_SEL_BLOCK
    jj = jnp.arange(n_slc)[None, :]
    forced = (jj == 0) | (jj == cur[:, None]) | (jj == cur[:, None] - 1)
    imp = jnp.where(jj > cur[:, None], NEG, jnp.where(forced, FORCE, imp))
    n_sel = min(NSA_N_SEL, n_slc)
    _, sidx = lax.top_k(imp, n_sel)
    ksb = k_sel.transpose(0, 2, 1, 3).reshape(B, H_NSA_KV, n_slc, NSA_SEL_BLOCK, hd)
    vsb = v_sel.transpose(0, 2, 1, 3).reshape(B, H_NSA_KV, n_slc, NSA_SEL_BLOCK, hd)
    C = NSA_QCHUNK
    nc = S // C
    qt = qg.transpose(0, 2, 3, 1, 4)
    qc = qt.reshape(B, H_NSA_KV, NSA_GROUP, nc, C, hd).transpose(3, 0, 1, 2, 4, 5)
    ic = sidx.reshape(B, H_NSA_KV, nc, C, n_sel).transpose(2, 0, 1, 3, 4)
    tc = tpos.reshape(nc, C)
    bi = jnp.arange(B)[:, None, None, None]
    ki = jnp.arange(H_NSA_KV)[None, :, None, None]
    offs = jnp.arange(NSA_SEL_BLOCK)
    m_sel = n_sel * NSA_SEL_BLOCK
    def chunk(args):
        qi, ii, ti = args
        kg = ksb[bi, ki, ii]
        vg = vsb[bi, ki, ii]
        s = jnp.einsum('bkgcd,bkcnld->bkgcnl', qi, kg).astype(jnp.float32) * SCALE
        kpos = ii[..., None] * NSA_SEL_BLOCK + offs
        mask = (kpos <= ti[None, None, :, None, None]).reshape(B, H_NSA_KV, 1, C, m_sel)
        p, _ = masked_softmax(s.reshape(B, H_NSA_KV, NSA_GROUP, C, m_sel), mask)
        return jnp.einsum('bkgcm,bkcmd->bkgcd', p.astype(vg.dtype), vg.reshape(B, H_NSA_KV, C, m_sel, hd))
    o_sel = lax.map(chunk, (qc, ic, tc))
    o_sel = o_sel.transpose(1, 0, 4, 2, 3, 5).reshape(B, S, H_NSA_KV, NSA_GROUP, hd)
    o_win, _ = banded_attention(qt, k_win.transpose(0, 2, 1, 3), v_win.transpose(0, 2, 1, 3), NSA_WINDOW - 1, BAND_BLOCK)
    o_win = o_win.transpose(0, 3, 1, 2, 4)
    g = jax.nn.sigmoid(gate_logits.astype(jnp.float32)).reshape(B, S, H_NSA_KV, NSA_GROUP, 3).astype(q.dtype)
    out = g[..., 0:1] * o_cmp + g[..., 1:2] * o_sel + g[..., 2:3] * o_win
    return out.reshape(B, S, H_NSA * hd)


def dilated_group(q, k, v, dil, max_dist):
    B, S, H, hd = q.shape
    L = S // dil
    Lp = -(-L // BAND_BLOCK) * BAND_BLOCK
    def to_classes(t):
        t = t.reshape(B, L, dil, H, hd).transpose(0, 2, 3, 1, 4).reshape(B * dil, H, L, hd)
        return jnp.pad(t, ((0, 0), (0, 0), (0, Lp - L), (0, 0)))
    o, lse = banded_attention(to_classes(q)[:, :, None], to_classes(k), to_classes(v), max_dist, BAND_BLOCK)
    o = o[:, :, 0, :L].reshape(B, dil, H, L, hd).transpose(0, 3, 1, 2, 4).reshape(B, S, H, hd)
    lse = lse[:, :, 0, :L].reshape(B, dil, H, L).transpose(0, 3, 1, 2).reshape(B, S, H)
    return o, lse


def dilated_mixer(q, k, v):
    B, S, _, hd = q.shape
    outs, lses = [], []
    for g, (window, dil) in enumerate(DIL_CFG):
        sl = slice(g * DIL_HEADS_PER_GROUP, (g + 1) * DIL_HEADS_PER_GROUP)
        o, lse = dilated_group(q[:, :, sl], k[:, :, sl], v[:, :, sl], dil, window // dil)
        outs.append(o)
        lses.append(lse)
    alpha = jax.nn.softmax(jnp.stack(lses, axis=2), axis=2)
    o = jnp.stack(outs, axis=2) * alpha[..., None].astype(q.dtype)
    return o.reshape(B, S, H_DIL * hd)


def conv_ffn(h, w_gate, w_up, conv_w, conv_b, w_down):
    g = h @ w_gate
    u = h @ w_up
    g = lax.conv_general_dilated(g, conv_w[:, None, :], window_strides=(1,), padding=[(CONV_WIDTH - 1, 0)],
                                 dimension_numbers=('NWC', 'WIO', 'NWC'), feature_group_count=D_FF) + conv_b
    return (jax.nn.silu(g) * u) @ w_down


def hybrid_layer(x, ln1, w_in, qn_a, kn_a, qn_b, kn_b, cmp_pe_k, cmp_pe_v, cmp_k_w1, cmp_k_w2, cmp_v_w1, cmp_v_w2,
                 qn_c, kn_c, w_out, ln2, w_gate, w_up, conv_w, conv_b, w_down):
    B, S, _ = x.shape
    h = rms_norm(x, ln1)
    proj = h @ w_in
    (qa, ka, va, qb, kcb, vcb, ksb, vsb, kwb, vwb, gb, qc, kc, vc) = jnp.split(
        proj, np.cumsum(IN_SPLITS)[:-1].tolist(), axis=-1)
    heads = lambda t, n: t.reshape(B, S, n, HEAD_DIM)
    pos = jnp.arange(S)[:, None]
    qk = lambda t, n, g: apply_rope(rms_norm(heads(t, n), g), pos)
    o_a = moba_mixer(qk(qa, H_MOBA, qn_a), qk(ka, H_MOBA, kn_a), heads(va, H_MOBA))
    o_b = nsa_mixer(qk(qb, H_NSA, qn_b), heads(kcb, H_NSA_KV), heads(vcb, H_NSA_KV),
                    qk(ksb, H_NSA_KV, kn_b[1]), heads(vsb, H_NSA_KV), qk(kwb, H_NSA_KV, kn_b[2]), heads(vwb, H_NSA_KV),
                    gb, kn_b[0], cmp_pe_k, cmp_pe_v, cmp_k_w1, cmp_k_w2, cmp_v_w1, cmp_v_w2)
    o_c = dilated_mixer(qk(qc, H_DIL, qn_c), qk(kc, H_DIL, kn_c), heads(vc, H_DIL))
    x = x + jnp.concatenate([o_a, o_b, o_c], axis=-1) @ w_out
    return x + conv_ffn(rms_norm(x, ln2), w_gate, w_up, conv_w, conv_b, w_down)


def setup_inputs(seed: int = 0) -> dict:
    key = jax.random.key(seed)
    ks = jax.random.split(key, 24)
    f32 = jnp.float32
    nrm = lambda k, shape, scale: jax.random.normal(k, shape, f32) * scale
    gain = lambda k, shape: 1.0 + 0.05 * jax.random.normal(k, shape, f32)
    flat = NSA_CMP_LEN * HEAD_DIM
    return {
        'x': nrm(ks[0], (BATCH, SEQ, D_MODEL), 1.0),
        'ln1': gain(ks[1], (DEPTH, D_MODEL)),
        'w_in': nrm(ks[2], (DEPTH, D_MODEL, D_IN), D_MODEL ** -0.5),
        'qn_a': gain(ks[3], (DEPTH, HEAD_DIM)),
        'kn_a': gain(ks[4], (DEPTH, HEAD_DIM)),
        'qn_b': gain(ks[5], (DEPTH, HEAD_DIM)),
        'kn_b': gain(ks[6], (DEPTH, 3, HEAD_DIM)),
        'cmp_pe_k': nrm(ks[7], (DEPTH, NSA_CMP_LEN, HEAD_DIM), 0.1),
        'cmp_pe_v': nrm(ks[8], (DEPTH, NSA_CMP_LEN, HEAD_DIM), 0.1),
        'cmp_k_w1': nrm(ks[9], (DEPTH, flat, NSA_CMP_HIDDEN), flat ** -0.5),
        'cmp_k_w2': nrm(ks[10], (DEPTH, NSA_CMP_HIDDEN, HEAD_DIM), NSA_CMP_HIDDEN ** -0.5),
        'cmp_v_w1': nrm(ks[11], (DEPTH, flat, NSA_CMP_HIDDEN), flat ** -0.5),
        'cmp_v_w2': nrm(ks[12], (DEPTH, NSA_CMP_HIDDEN, HEAD_DIM), NSA_CMP_HIDDEN ** -0.5),
        'qn_c': gain(ks[13], (DEPTH, HEAD_DIM)),
        'kn_c': gain(ks[14], (DEPTH, HEAD_DIM)),
        'w_out': nrm(ks[15], (DEPTH, D_MODEL, D_MODEL), D_MODEL ** -0.5),
        'ln2': gain(ks[16], (DEPTH, D_MODEL)),
        'w_gate': nrm(ks[17], (DEPTH, D_MODEL, D_FF), D_MODEL ** -0.5),
        'w_up': nrm(ks[18], (DEPTH, D_MODEL, D_FF), D_MODEL ** -0.5),
        'conv_w': nrm(ks[19], (DEPTH, CONV_WIDTH, D_FF), CONV_WIDTH ** -0.5),
        'conv_b': nrm(ks[20], (DEPTH, D_FF), 0.01),
        'w_down': nrm(ks[21], (DEPTH, D_FF, D_MODEL), D_FF ** -0.5),
    }


def reference(x, ln1, w_in, qn_a, kn_a, qn_b, kn_b, cmp_pe_k, cmp_pe_v, cmp_k_w1, cmp_k_w2, cmp_v_w1, cmp_v_w2,
              qn_c, kn_c, w_out, ln2, w_gate, w_up, conv_w, conv_b, w_down):
    for l in range(DEPTH):
        x = hybrid_layer(x, ln1[l], w_in[l], qn_a[l], kn_a[l], qn_b[l], kn_b[l], cmp_pe_k[l], cmp_pe_v[l],
                         cmp_k_w1[l], cmp_k_w2[l], cmp_v_w1[l], cmp_v_w2[l], qn_c[l], kn_c[l], w_out[l],
                         ln2[l], w_gate[l], w_up[l], conv_w[l], conv_b[l], w_down[l])
    return x
```

```python
import numpy as np
import ml_dtypes
from contextlib import ExitStack
import concourse.bass as bass
import concourse.mybir as mybir
from concourse.bass_utils import run_bass_kernel_spmd

F32 = mybir.dt.float32
BF16 = mybir.dt.bfloat16
ALU = mybir.AluOpType
AF = mybir.ActivationFunctionType
AX = mybir.AxisListType

S = 2048
D = 1024
NT = 16
DFF = 2816
NFC = 22
DIN = 3090
SCALE = 0.125
EPS = 1e-6
BIG = 32768.0
C_QA, C_KA, C_VA = 0, 256, 512
C_QB, C_KCB, C_VCB, C_KSB, C_VSB, C_KWB, C_VWB, C_GB = 768, 1152, 1280, 1408, 1536, 1664, 1792, 1920
C_QC, C_KC, C_VC = 1938, 2322, 2706

EPOCH = 24000
NSLOT = 8


class Buf:
    __slots__ = ("name", "w", "r")

    def __init__(self, name):
        self.name = name
        self.w = None
        self.r = {}


class Prog:
    ENGS = ("pe", "act", "dve", "pool")
    DMAQ = ("sp", "pool", "act")

    def __init__(self, nc):
        self.nc = nc
        self.streams = {k: [] for k in ("pe", "act", "dve", "pool", "sp")}
        self.cnt = {k: 0 for k in self.ENGS}
        self.esems = {k: [] for k in self.ENGS}
        self.dcnt = {k: 0 for k in self.DMAQ}
        self.dsems = {k: [nc.alloc_semaphore(name=f"d_{k}_{i}") for i in range(NSLOT)] for k in self.DMAQ}
        self.clock = {k: {} for k in self.streams}
        self.nbuf = 0

    def buf(self, name=None):
        self.nbuf += 1
        return Buf(name or f"b{self.nbuf}")

    def _esem(self, eng, seq):
        ep = (seq - 1) // EPOCH
        while len(self.esems[eng]) <= ep:
            self.esems[eng].append(self.nc.alloc_semaphore(name=f"e_{eng}_{len(self.esems[eng])}"))
        return self.esems[eng][ep], seq - ep * EPOCH

    def _tok_wait(self, tok):
        if tok[0] == "e":
            sem, val = self._esem(tok[1], tok[2])
            ep = (tok[2] - 1) // EPOCH
            return ("e", tok[1], ep), val, sem, val
        _, q, idx = tok
        slot = idx % NSLOT
        val = 16 * (idx // NSLOT + 1)
        return ("d", q, slot), val, self.dsems[q][slot], val

    def _waits_for(self, stream, toks):
        waits = []
        clk = self.clock[stream]
        for t in toks:
            if t[0] == "e" and t[1] == "pe" and stream == "pe":
                continue
            key, lvl, sem, val = self._tok_wait(t)
            if t[0] == "e":
                done = False
                for (k2, l2) in list(clk.items()):
                    if k2[0] == "e" and k2[1] == t[1] and k2[2] > key[2] and l2 > 0:
                        done = True
                if done:
                    continue
            if clk.get(key, 0) >= lvl:
                continue
            clk[key] = lvl
            waits.append((sem, val))
        return waits

    def _collect(self, stream, reads, writes):
        toks = set()
        for b in reads:
            if b.w is not None:
                toks.add(b.w)
        for b in writes:
            if b.w is not None:
                toks.add(b.w)
            for t in b.r.values():
                toks.add(t)
        return self._waits_for(stream, toks)

    def _mark(self, stream, tok, reads, writes):
        for b in writes:
            b.w = tok
            b.r = {}
        rk = tok if tok[0] == "d" else stream
        for b in reads:
            if b not in writes:
                b.r[rk] = tok

    def op(self, eng, fn, r=(), w=()):
        r = [b for b in r if b is not None]
        w = [b for b in w if b is not None]
        waits = self._collect(eng, r, w)
        self.cnt[eng] += 1
        seq = self.cnt[eng]
        sem, _ = self._esem(eng, seq)
        self.streams[eng].append((fn, waits, sem, 1))
        self._mark(eng, ("e", eng, seq), r, w)

    def dma(self, q, fn, r=(), w=()):
        r = [b for b in r if b is not None]
        w = [b for b in w if b is not None]
        waits = self._collect(q, r, w)
        idx = self.dcnt[q]
        self.dcnt[q] += 1
        slot = idx % NSLOT
        if idx >= NSLOT:
            key = ("d", q, slot)
            lvl = 16 * (idx // NSLOT)
            if self.clock[q].get(key, 0) < lvl:
                self.clock[q][key] = lvl
                waits.append((self.dsems[q][slot], lvl))
        self.streams[q].append((fn, waits, self.dsems[q][slot], 16))
        self._mark(q, ("d", q, idx), r, w)

    def barrier(self, skip_queues=()):
        toks = []
        for e in self.ENGS:
            if self.cnt[e] > 0:
                toks.append(("e", e, self.cnt[e]))
        for q in self.DMAQ:
            if q in skip_queues:
                continue
            n = self.dcnt[q]
            for idx in range(max(0, n - NSLOT), n):
                toks.append(("d", q, idx))
        for s in self.streams:
            tk = [t for t in toks if not (t[0] == "e" and t[1] == s and s == "pe")]
            waits = self._waits_for(s, tk)
            if waits:
                self.streams[s].append((None, waits, None, 0))

    def final_wait(self, stream, bufs):
        waits = self._collect(stream, bufs, [])
        self.streams[stream].append((None, waits, None, 0))

    def replay(self):
        nc = self.nc
        with nc.Block() as block:
            def run(engine, items):
                for fn, waits, sem, inc in items:
                    for s, v in waits:
                        engine.wait_ge(s, v)
                    if fn is not None:
                        fn(engine).then_inc(sem, inc)

            @block.tensor
            def _(e):
                run(e, self.streams["pe"])

            @block.scalar
            def _(e):
                run(e, self.streams["act"])

            @block.vector
            def _(e):
                run(e, self.streams["dve"])

            @block.gpsimd
            def _(e):
                run(e, self.streams["pool"])

            @block.sync
            def _(e):
                run(e, self.streams["sp"])


class Ring:
    def __init__(self, items):
        self.items = items
        self.i = 0

    def next(self):
        it = self.items[self.i % len(self.items)]
        self.i += 1
        return it


def _bf(a):
    return np.ascontiguousarray(a).astype(ml_dtypes.bfloat16)


def make_consts():
    c = {}
    c["ident_f"] = np.eye(32, dtype=np.float32)
    c["ident_b"] = _bf(np.eye(128))
    obd = np.zeros((128, 128), np.float32)
    obd[0:64, 0:64] = 1.0
    obd[64:128, 64:128] = 1.0
    c["onesBD"] = _bf(obd)
    R = np.zeros((64, 64), np.float32)
    for i in range(8):
        R[i + 8, i] = -1.0
        R[i, i + 8] = 1.0
    rbd = np.zeros((128, 128), np.float32)
    rbd[0:64, 0:64] = R
    rbd[64:128, 64:128] = R
    c["rotBD"] = _bf(rbd)
    half = 8
    inv = (500000.0 ** (-(np.arange(half, dtype=np.float32) * 2.0 / 16.0))).astype(np.float32)
    pos = np.arange(S, dtype=np.float32)
    ang = (pos[None, :] * inv[:, None]).astype(np.float32)
    C = np.ones((128, S), np.float32)
    Sn = np.zeros((128, S), np.float32)
    for o in (0, 64):
        C[o:o + 8] = np.cos(ang)
        C[o + 8:o + 16] = np.cos(ang)
        Sn[o:o + 8] = np.sin(ang)
        Sn[o + 8:o + 16] = np.sin(ang)
    c["ropeC"] = _bf(C)
    c["ropeS"] = _bf(Sn)
    b = np.arange(128)[:, None]
    a = np.arange(128)[None, :]
    c["m_causal"] = _bf(np.where(b <= a, 0.0, -BIG))
    c["m_le"] = _bf(np.where(a <= b, 0.0, -BIG))
    c["m_lt"] = _bf(np.where(a < b, 0.0, -BIG))
    n = np.arange(128)[:, None, None]
    cc = np.arange(4)[None, :, None]
    tl = np.arange(512)[None, None, :]
    c["m_cmp"] = _bf(np.where((n <= 126) & (16 * n + 31 <= 512 * cc + tl), 0.0, -BIG))
    k = np.arange(S)[None, :]
    c["ka_moba"] = _bf((k // 256 == np.arange(8)[:, None]).astype(np.float32))
    c["ka_sel"] = _bf((k // 64 == np.arange(32)[:, None]).astype(np.float32))
    for nm, dil in (("d4", 4), ("d16", 16)):
        ka = np.zeros((dil + 1, S), np.float32)
        ka[:dil] = (k % dil == np.arange(dil)[:, None])
        ka[dil] = -BIG
        qa = np.zeros((dil + 1, 512), np.float32)
        qa[:dil] = BIG * (np.arange(512)[None, :] % dil == np.arange(dil)[:, None])
        qa[dil] = 1.0
        c["ka_" + nm] = _bf(ka)
        c["qa_" + nm] = _bf(qa)
        kb = np.zeros((64, S), np.float32)
        kb[:dil + 1] = ka
        qb = np.zeros((64, 512), np.float32)
        qb[:dil + 1] = qa
        c["kb_" + nm] = _bf(kb)
        c["qb_" + nm] = _bf(qb)
    kbm = np.zeros((64, S), np.float32)
    kbm[:8] = (k // 256 == np.arange(8)[:, None])
    c["kb_moba"] = _bf(kbm)
    kbs = np.zeros((64, S), np.float32)
    kbs[:32] = (k // 64 == np.arange(32)[:, None])
    c["kb_sel"] = _bf(kbs)
    p = np.arange(128)[:, None, None]
    i = np.arange(16)[None, :, None]
    t = 128 * i + p
    nb = np.arange(8)[None, None, :]
    cur = t // 256
    c["mb_T"] = _bf(np.where(nb < cur, 0.0, -1e30))
    c["mb_own"] = _bf((nb == cur).astype(np.float32))
    j = np.arange(32)[None, None, :]
    cur = t // 64
    forced = (j == 0) | (j == cur) | (j == cur - 1)
    vis = (j <= cur)
    c["ns_A"] = _bf(((~forced) & vis).astype(np.float32))
    c["ns_B"] = _bf(np.where(vis, np.where(forced, 1e9, 0.0), -1e30))
    starts = np.arange(127) * 16
    jj = np.arange(32)
    ov = (starts[:, None] < (jj[None, :] + 1) * 64) & (starts[:, None] + 32 > jj[None, :] * 64)
    ova = np.zeros((128, 33), np.float32)
    ova[:127, 0] = 1.0
    ova[:127, 1:] = ov
    c["ovl"] = _bf(ova)
    return c


CONST_SPECS = None


def build(NB=2, NL=2, dbg=None):
    nc = bass.Bass("TRN2", target_bir_lowering=False)
    P = Prog(nc)
    consts = make_consts()

    def din(name, shape, dt=F32):
        return nc.dram_tensor(name, list(shape), dt, kind="ExternalInput").ap()

    x_d = din("x", [NB, S, D])
    y_d = nc.dram_tensor("y", [NB, S, D], F32, kind="ExternalOutput").ap()
    W = {}
    for nm, shp in (("ln1", [2, D]), ("w_in", [2, D, DIN]), ("qn_a", [2, 64]), ("kn_a", [2, 64]), ("qn_b", [2, 64]),
                    ("kn_b", [2, 3, 64]), ("cmp_pe_k", [2, 32, 64]), ("cmp_pe_v", [2, 32, 64]),
                    ("cmp_k_w1", [2, 2048, 128]), ("cmp_k_w2", [2, 128, 64]), ("cmp_v_w1", [2, 2048, 128]),
                    ("cmp_v_w2", [2, 128, 64]), ("qn_c", [2, 64]), ("kn_c", [2, 64]), ("w_out", [2, D, D]),
                    ("ln2", [2, D]), ("w_gate", [2, D, DFF]), ("w_up", [2, D, DFF]), ("conv_w", [2, 3, DFF]),
                    ("conv_b", [2, DFF]), ("w_down", [2, DFF, D])):
        W[nm] = din(nm, shp)
    CD = {}
    for nm, arr in consts.items():
        CD[nm] = din("c_" + nm, arr.shape, BF16 if arr.dtype == ml_dtypes.bfloat16 else F32)
    dbg_d = None
    if dbg is not None:
        dbg_d = nc.dram_tensor("dbg", list(dbg), F32, kind="ExternalOutput").ap()
    ybufs = [P.buf(f"y{i}") for i in range(NT)]

    root = ExitStack()

    uid = [0]

    def sb(es, name, shape, dt):
        uid[0] += 1
        t = es.enter_context(nc.sbuf_tensor(f"{name}_{uid[0]}", list(shape), dt))
        return t, P.buf(name)

    def psb(es, name, shape, dt):
        t = es.enter_context(nc.psum_tensor(name, list(shape), dt))
        return t, P.buf(name)

    def mm(out, lhsT, rhs, start, stop, r, w):
        P.op("pe", lambda e: e.matmul(out, lhsT=lhsT, rhs=rhs, start=start, stop=stop, skip_group_check=True), r=r, w=w)

    def tr(out, in_, ident, r, w):
        P.op("pe", lambda e: e.transpose(out=out, in_=in_, identity=ident), r=r, w=w)

    def act(out, in_, func, r, w, scale=1.0, bias=None, accum_out=None):
        kw = {}
        if bias is not None:
            kw["bias"] = bias
        if accum_out is not None:
            kw["accum_out"] = accum_out
        P.op("act", lambda e: e.activation(out=out, in_=in_, func=func, scale=scale, **kw), r=r, w=w)

    def tt(out, in0, in1, op, r, w):
        P.op("dve", lambda e: e.tensor_tensor(out=out, in0=in0, in1=in1, op=op), r=r, w=w)

    def ts(out, in0, s1, s2, op0, op1, r, w):
        if s2 is None:
            nm = {ALU.mult: "tensor_scalar_mul", ALU.add: "tensor_scalar_add", ALU.max: "tensor_scalar_max"}[op0]
            P.op("dve", lambda e: getattr(e, nm)(out=out, in0=in0, scalar1=s1), r=r, w=w)
        else:
            P.op("dve", lambda e: e.tensor_scalar(out=out, in0=in0, scalar1=s1, scalar2=s2, op0=op0, op1=op1), r=r, w=w)

    def stt(out, in0, scalar, in1, op0, op1, r, w):
        P.op("dve", lambda e: e.scalar_tensor_tensor(out=out, in0=in0, scalar=scalar, in1=in1, op0=op0, op1=op1), r=r, w=w)

    def cp(out, in_, r, w):
        P.op("dve", lambda e: e.tensor_copy(out=out, in_=in_), r=r, w=w)

    def recip(out, in_, r, w):
        P.op("dve", lambda e: e.reciprocal(out=out, in_=in_), r=r, w=w)

    def memset(ap, val, w):
        P.op("dve", lambda e: e.memset(ap, val), w=w)

    def dma(q, out, in_, r, w, slow=False):
        if slow:
            P.dma(q, lambda e: e.dma_start(out=out, in_=in_, allow_slow_non_contiguous=True), r=r, w=w)
        else:
            P.dma(q, lambda e: e.dma_start(out=out, in_=in_), r=r, w=w)

    xs, _ = sb(root, "xs", [128, NT, D], F32)
    xb = [P.buf(f"x{i}") for i in range(NT)]
    hb = [P.buf(f"h{i}") for i in range(4)]
    HT = {}
    for i in range(4):
        dma("sp", xs[:, i, :], x_d[0, i * 128:(i + 1) * 128, :], r=[], w=[xb[i]])
    CT = {}
    cbuf = P.buf("consts")
    ebuf = P.buf("early_consts")
    EARLY = ("ident_f", "ident_b")
    for nm, arr in consts.items():
        shp = list(arr.shape)
        if nm[:3] in ("ka_", "qa_", "kb_", "qb_"):
            continue
        CT[nm], _ = sb(root, "k_" + nm, shp, BF16 if arr.dtype == ml_dtypes.bfloat16 else F32)
    for nm in EARLY:
        dma("sp", CT[nm][:], CD[nm], r=[], w=[ebuf])

    def load_rest_consts():
        for nm in CT:
            if nm not in EARLY:
                dma("sp", CT[nm][:], CD[nm], r=[], w=[cbuf])
    cb = [cbuf, ebuf]
    wring = Ring([sb(root, f"wst{i}", [128, 8, 256], BF16) for i in range(2)])
    HG = {"qn_a": 0, "kn_a": 1, "qn_b": 2, "kn_cmp": 3, "kn_sel": 4, "kn_win": 5, "qn_c": 6, "kn_c": 7}

    ps_proj = Ring([psb(root, f"psA{i}", [128, 512], F32) for i in range(2)])
    ps_s = Ring([psb(root, f"psS{i}", [128, 512], F32) for i in range(2)])
    ps_pv = Ring([psb(root, f"psV{i}", [128, 512], F32) for i in range(2)])
    ps_aux = Ring([psb(root, f"psX{i}", [128, 512], F32) for i in range(2)])
    RG = {'proj': ps_proj, 's': ps_s, 'aux': ps_aux}
    ALT = {'proj': Ring([ps_proj.items[0], ps_proj.items[1]]), 'aux': Ring([ps_aux.items[0]]),
           's': Ring([ps_s.items[0], ps_s.items[1], ps_aux.items[1]])}

    def rings_alt(on):
        RG['proj'] = ALT['proj'] if on else ps_proj
        RG['s'] = ALT['s'] if on else ps_s
        RG['aux'] = ALT['aux'] if on else ps_aux

    tmp = ExitStack()
    sq_r = Ring([sb(root, f"sq{i}", [128, 512], BF16) for i in range(2)])
    qg_r = Ring([sb(root, f"qg{i}", [128, 512], BF16) for i in range(2)])
    f32_items = [sb(root, f"f32t{i}", [128, 516], F32) for i in range(4)]
    f32_r = Ring(f32_items)
    pt_r = Ring([sb(root, f"pt{i}", [128, 512], BF16) for i in range(3)])
    xn_r = Ring([sb(root, f"xn{i}", [128, D], BF16) for i in range(1)])
    junk, junkb = xn_r.items[0]
    ssq, ssqb = sb(root, "ssq", [128, NT], F32)
    rstd, rstdb = sb(root, "rstd", [128, NT], F32)

    ident_b = CT["ident_b"]
    ident_f = CT["ident_f"]

    gains, gbuf = sb(root, "gains", [128, 2, 2, 8], F32)
    gst, gstb = f32_items[0]
    for l in range(2):
        for k, nm in enumerate(("ln1", "ln2")):
            r0 = (l * 2 + k) * 8
            dma("sp", gst[r0:r0 + 8, 0:128], W[nm][l].rearrange("(c p) -> c p", p=128), r=[], w=[gstb])
    pa_, pab_ = RG['aux'].next()
    tr(pa_[:, 0:32], gst[0:32, 0:128], ident_f[0:32, 0:32], r=[gstb, ebuf], w=[pab_])
    cp(gains[:].rearrange("p l k c -> p (l k c)"), pa_[:, 0:32], r=[pab_], w=[gbuf])
    load_rest_consts()
    hg, hgbuf = sb(root, "hgains", [128, 2, 8], F32)
    hst, hstb = f32_items[1]
    for l in range(2):
        for k, nm in ((0, "qn_a"), (1, "kn_a"), (2, "qn_b"), (6, "qn_c"), (7, "kn_c")):
            dma("sp", hst[l * 8 + k:l * 8 + k + 1, 0:64], W[nm][l:l + 1, :], r=[], w=[hstb])
        dma("sp", hst[l * 8 + 3:l * 8 + 6, 0:64], W["kn_b"][l], r=[], w=[hstb])
    dma("sp", hst[0:16, 64:128], hst[0:16, 0:64], r=[hstb], w=[hstb])
    HGL = [False]

    def hg_once():
        if HGL[0]:
            return
        HGL[0] = True
        pa2, pab2 = RG['aux'].next()
        tr(pa2[:, 0:16], hst[0:16, 0:128], ident_f[0:16, 0:16], r=[hstb, ebuf], w=[pab2])
        cp(hg[:].rearrange("p l k -> p (l k)"), pa2[:, 0:16], r=[pab2], w=[hgbuf])
    cw_all_t, cwallb = sb(root, "cw_all", [128, 2, 4, NFC], F32)
    cw_all = cw_all_t[:]
    CWL = [False]

    def load_conv_once():
        if CWL[0]:
            return
        CWL[0] = True
        for l in range(2):
            for k in range(3):
                dma("sp", cw_all[:, l, k, :], W["conv_w"][l, k].rearrange("(fc p) -> p fc", p=128), r=[], w=[cwallb], slow=True)
            dma("sp", cw_all[:, l, 3, :], W["conv_b"][l].rearrange("(fc p) -> p fc", p=128), r=[], w=[cwallb], slow=True)


    def wview_in(l, c0, width):
        return W["w_in"][l].rearrange("(kc p) n -> p kc n", p=128)[:, :, c0:c0 + width]

    def load_w(dst, dbuf, src):
        dma("pool", dst, src, r=[], w=[dbuf])

    def norm_T(l, which, tiles, dst, dst_bufs_of_tile, dst_col_of_tile):
        t0, t1 = tiles[0], tiles[-1] + 1
        memset(ssq[:, t0:t1], 0.0, w=[ssqb])
        for i in tiles:
            act(junk[:], xs[:, i, :], AF.Square, r=[xb[i]], w=[junkb, ssqb], accum_out=ssq[:, i:i + 1])
        act(rstd[:, t0:t1], ssq[:, t0:t1], AF.Ln, r=[ssqb], w=[rstdb], scale=1.0 / D, bias=EPS)
        act(rstd[:, t0:t1], rstd[:, t0:t1], AF.Exp, r=[rstdb], w=[rstdb], scale=-0.5)
        for i in tiles:
            xn, xnb = xn_r.next()
            act(xn[:], xs[:, i, :], AF.Copy, r=[xb[i], rstdb], w=[xnb], scale=rstd[:, i:i + 1])
            pa, pab = RG['aux'].next()
            pav = pa[:].bitcast(BF16).rearrange("p (c t) -> p c t", c=8)
            for c in range(8):
                tr(pav[:, c, :], xn[:, c * 128:(c + 1) * 128], ident_b[:], r=[xnb, ebuf], w=[pab])
            col = dst_col_of_tile(i)
            tt(dst[:, :, col:col + 128], pav, gains[:, l, which, :].unsqueeze(2).to_broadcast([128, 8, 128]), ALU.mult,
               r=[pab, gbuf], w=[dst_bufs_of_tile(i)])

    def finish_pair(l, pa, pab, ntok, gain_idx, dstA, dstAb, dstB, dstBb, dcol0, tok0, tabC=None, tabS=None):
        sq, sqb = sq_r.next()
        qg, qgb = qg_r.next()
        act(sq[:, 0:ntok], pa[:, 0:ntok], AF.Square, r=[pab], w=[sqb])
        act(qg[:, 0:ntok], pa[:, 0:ntok], AF.Copy, r=[pab, hgbuf], w=[qgb], scale=hg[:, l, gain_idx:gain_idx + 1])
        px, pxb = pa, pab
        mm(px[:, 0:ntok], CT["onesBD"][:], sq[:, 0:ntok], True, True, r=[sqb, qgb] + cb, w=[pxb])
        py, pyb = RG['aux'].next()
        mm(py[:, 0:ntok], CT["rotBD"][:], qg[:, 0:ntok], True, True, r=[qgb] + cb, w=[pyb])
        rs, rsb = f32_r.next()
        act(rs[:, 0:ntok], px[:, 0:ntok], AF.Ln, r=[pxb], w=[rsb], scale=1.0 / 64, bias=EPS)
        act(rs[:, 0:ntok], rs[:, 0:ntok], AF.Exp, r=[rsb], w=[rsb], scale=-0.5)
        if tabC is None:
            tabC = CT["ropeC"][:, tok0:tok0 + ntok]
            tabS = CT["ropeS"][:, tok0:tok0 + ntok]
        t1, t1b = f32_r.next()
        t2, t2b = f32_r.next()
        tt(t1[:, 0:ntok], qg[:, 0:ntok], tabC, ALU.mult, r=[qgb] + cb, w=[t1b])
        tt(t2[:, 0:ntok], py[:, 0:ntok], tabS, ALU.mult, r=[pyb] + cb, w=[t2b])
        tt(t1[:, 0:ntok], t1[:, 0:ntok], t2[:, 0:ntok], ALU.add, r=[t1b, t2b], w=[t1b])
        if dstA is dstB:
            tt(dstA[:, dcol0:dcol0 + ntok], t1[:, 0:ntok], rs[:, 0:ntok], ALU.mult, r=[t1b, rsb], w=[dstAb])
        else:
            tt(dstA[0:64, dcol0:dcol0 + ntok], t1[0:64, 0:ntok], rs[0:64, 0:ntok], ALU.mult, r=[t1b, rsb], w=[dstAb])
            tt(dstB[64:128, dcol0:dcol0 + ntok], t1[64:128, 0:ntok], rs[64:128, 0:ntok], ALU.mult, r=[t1b, rsb], w=[dstBb])

    def v_tm(wt, wtb, wcol, nh, vt, vtb, slot0):
        for i in range(NT):
            pa, pab = RG['proj'].next()
            for kc in range(8):
                mm(pa[:, 0:nh * 64], HT['t'][:, kc, i * 128:(i + 1) * 128], wt[:, kc, wcol:wcol + nh * 64], kc == 0, kc == 7,
                   r=[wtb, hb[i // 4]], w=[pab])
            act(vt[:, i, slot0:slot0 + nh, 0:64], pa[:, 0:nh * 64].rearrange("p (h d) -> p h d", h=nh), AF.Copy,
                r=[pab], w=[vtb])

    WARMN = [0]

    def attn_units(specs, c):
        items = []
        for sp in specs:
            lst = sp["tiles"](c)
            for n, (ki, a0, a1, masks) in enumerate(lst):
                items.append((sp, ki, a0, a1, masks, n == 0, n == len(lst) - 1))
        state = {}

        def qk(it):
            sp, ki, a0, a1, masks, isf, isl = it
            kparts = sp.get("kparts", 128)
            r0, r1 = sp["rows"]
            qt_, qtb = sp["q"]
            kt_, ktb = sp["k"]
            s_, sbf = RG['s'].next()
            nmask = len(masks)
            mm(s_[0:kparts, a0 * 128:a1 * 128], kt_[r0:r1, ki * 128:ki * 128 + kparts], qt_[r0:r1, a0 * 128:a1 * 128],
               True, nmask == 0, r=[ktb, qtb], w=[sbf])
            for mi, (j, mname) in enumerate(masks):
                msrc = CT[mname][0:kparts, :] if isinstance(mname, str) else mname
                mm(s_[0:kparts, j * 128:(j + 1) * 128], ident_b[0:kparts, 0:kparts], msrc, False, mi == nmask - 1,
                   r=cb, w=[sbf])
            return s_, sbf

        def rest(it, sres):
            sp, ki, a0, a1, masks, isf, isl = it
            s_, sbf = sres
            kparts = sp.get("kparts", 128)
            vcols = sp.get("vcols", 65)
            pvstride = sp.get("pvstride", 65)
            if isf:
                state["pv"] = ps_pv.next()
                state["started"] = False
            pv, pvb = state["pv"]
            pt, ptb = pt_r.next()
            act(pt[0:kparts, a0 * 128:a1 * 128], s_[0:kparts, a0 * 128:a1 * 128], AF.Exp, r=[sbf], w=[ptb], scale=SCALE)
            for j in range(a0, a1):
                rhs = sp["vt_ap"] if "vt_ap" in sp else sp["vt"][:, ki, sp["vslot"], 0:vcols]
                mm(pv[:, j * pvstride:j * pvstride + vcols], pt[0:kparts, j * 128:(j + 1) * 128], rhs,
                   not state["started"], False, r=[ptb, sp["vtb"]], w=[pvb])
                state["started"] = True
            if WARMN[0] > 0:
                jb, jbb = RG['proj'].next()
                mm(jb[:, 0:WARMN[0]], ident_b[:], pt[:, 0:WARMN[0]], True, True, r=[ptb] + cb, w=[jbb])
            if isl:
                sp["evac"](pv, pvb)

        units = []
        pend = []
        la = len(RG['s'].items) - 1

        def mk(it):
            def u():
                sres = qk(it)
                pend.append((it, sres))
                if len(pend) > la:
                    rest(*pend.pop(0))
            return u
        for it in items:
            units.append(mk(it))

        def last():
            while pend:
                rest(*pend.pop(0))
        units.append(last)
        return units

    def attn_multi(specs, c):
        for u in attn_units(specs, c):
            u()

    def run_interleaved(main, side):
        n, m = len(main), len(side)
        pos = {}
        for i in range(m):
            pos.setdefault(int((i + 0.5) * n / m), []).append(side[i])
        for k in range(n):
            for s_ in pos.get(k, []):
                s_()
            main[k]()
        for s_ in pos.get(n, []):
            s_()

    def pairs_units(l, jobs):
        box = [None]

        def mk(job):
            (wt, wtb, wcol, tok0, ntok, gain_idx, dA, dB, dcol0) = job

            def u():
                pa, pab = RG['proj'].next()
                for kc in range(8):
                    mm(pa[:, 0:ntok], wt[:, kc, wcol:wcol + 128], HT['t'][:, kc, tok0:tok0 + ntok], kc == 0, kc == 7,
                       r=[wtb, hb[tok0 // 512]], w=[pab])
                if box[0] is not None:
                    finish_pair(l, *box[0])
                box[0] = (pa, pab, ntok, gain_idx, dA[0], dA[1], dB[0], dB[1], dcol0, tok0)
            return u

        def last():
            if box[0] is not None:
                finish_pair(l, *box[0])
                box[0] = None
        return [mk(j) for j in jobs] + [last]

    def pairs_units_flat(l, jobs):
        units = []
        for job in jobs:
            (wt, wtb, wcol, tok0, ntok, gain_idx, dA, dB, dcol0) = job
            st = {}

            def pu(wt=wt, wtb=wtb, wcol=wcol, tok0=tok0, ntok=ntok, st=st):
                pa, pab = RG['proj'].next()
                for kc in range(8):
                    mm(pa[:, 0:ntok], wt[:, kc, wcol:wcol + 128], HT['t'][:, kc, tok0:tok0 + ntok], kc == 0, kc == 7,
                       r=[wtb, hb[tok0 // 512]], w=[pab])
                st['pa'] = (pa, pab)

            def fu(ntok=ntok, gain_idx=gain_idx, dA=dA, dB=dB, dcol0=dcol0, tok0=tok0, st=st):
                pa, pab = st['pa']
                finish_pair(l, pa, pab, ntok, gain_idx, dA[0], dA[1], dB[0], dB[1], dcol0, tok0)
            units += [pu, fu]
        return units

    def pairs_run(l, jobs):
        for u in pairs_units(l, jobs):
            u()

    def causal_tiles(c):
        out = []
        for ki in range(4 * c + 4):
            a0 = max(0, ki - 4 * c)
            masks = [(ki - 4 * c, "m_causal")] if ki >= 4 * c else []
            out.append((ki, a0, 4, masks))
        return out

    def band_tiles(maxd_tiles, far_mask, causal_only_class=False):
        def fn(c):
            out = []
            for ki in range(max(0, 4 * c - maxd_tiles), 4 * c + 4):
                a0 = max(0, ki - 4 * c)
                a1 = min(4, ki + maxd_tiles + 1 - 4 * c)
                if a1 <= a0:
                    continue
                masks = []
                if ki >= 4 * c:
                    masks.append((ki - 4 * c, "m_causal"))
                jf = ki + maxd_tiles - 4 * c
                if far_mask is not None and 0 <= jf < 4:
                    masks.append((jf, far_mask))
                out.append((ki, a0, a1, masks))
            return out
        return fn

    def outproj_units(l, c, ob, obb, ncc, wo, wob, oT, oTb):
        def tr_unit(j):
            def u():
                pa, pab = RG['aux'].next()
                pav = pa[:].bitcast(BF16).rearrange("p (c t) -> p c t", c=8)
                for cc in range(ncc):
                    tr(pav[:, cc, :], ob[:, j, cc * 128:(cc + 1) * 128], ident_b[:], r=[obb] + cb, w=[pab])
                cp(oT[:, 0:ncc, j * 128:(j + 1) * 128], pav[:, 0:ncc, :], r=[pab], w=[oTb])
            return u

        def mm_unit(j, hf):
            def u():
                i = 4 * c + j
                pa, pab = RG['proj'].next()
                for cc in range(ncc):
                    mm(pa[:, :], oT[:, cc, j * 128:(j + 1) * 128], wo[:, cc, hf * 512:(hf + 1) * 512], cc == 0, cc == ncc - 1,
                       r=[oTb, wob], w=[pab])
                tt(xs[:, i, hf * 512:(hf + 1) * 512], xs[:, i, hf * 512:(hf + 1) * 512], pa[:, :], ALU.add,
                   r=[pab, xb[i]], w=[xb[i]])
            return u
        return [tr_unit(j) for j in range(4)] + [mm_unit(j, hf) for j in range(4) for hf in range(2)]

    def out_proj(l, c, ob, obb, ncc, wo, wob, oT, oTb):
        for u in outproj_units(l, c, ob, obb, ncc, wo, wob, oT, oTb):
            u()

    def load_wo(es, l, row0, ncc):
        wo, wob = sb(es, "wo", [128, ncc, D], BF16)
        load_w(wo[:], wob, W["w_out"][l][row0:row0 + ncc * 128, :].rearrange("(cc p) n -> p cc n", p=128))
        return wo, wob

    def load_wcols(l, c0, width):
        wt, wtb = wring.next()
        load_w(wt[:, :, 0:width], wtb, wview_in(l, c0, width))
        return wt, wtb

    PH = []

    def moba(l):
        PH.append(('moba', dict(P.cnt)))
        pre_k = load_wcols(l, C_KA, 256)
        pre_v = load_wcols(l, C_VA, 256)
        with ExitStack() as es:
            kts = [sb(es, f"mk{h}", [128, S], BF16) for h in range(4)]
            ROWS = [(0, 72), (0, 128), (0, 72), (0, 128)]
            DR = [(0, 64), (64, 128), (0, 64), (64, 128)]
            vt, vtb = sb(es, "mv", [128, NT, 4, 65], BF16)
            memset(vt[:, :, :, 64:65], 1.0, w=[vtb])
            km, kmb = sb(es, "kmean", [128, 4, 8], F32)
            kmh, kmhb = sb(es, "kmeanb", [128, 4, 8], BF16)
            wq, wqb = sb(es, "wq", [128, 8, 256], BF16)
            load_w(wq[:], wqb, wview_in(l, C_QA, 256))
            wo, wob = load_wo(es, l, 0, 2)
            for h in (0, 2):
                dma("sp", kts[h][0][64:72, :], CD["ka_moba"], r=[], w=[kts[h][1]])
            for h in (1, 3):
                dma("sp", kts[h][0][0:64, :], CD["kb_moba"], r=[], w=[kts[h][1]])
            load_conv_once()
            hg_once()
            wt, wtb = pre_k
            pairs_run(l, [(wt, wtb, pr * 128, tc_ * 512, 512, HG["kn_a"], kts[2 * pr], kts[2 * pr + 1], tc_ * 512)
                          for pr in range(2) for tc_ in range(4)])
            for h in range(4):
                d0, d1 = DR[h]
                P.op("dve", lambda e, h=h, d0=d0, d1=d1: e.reduce_sum(out=km[d0:d1, h, :], in_=kts[h][0][d0:d1, :].rearrange("p (n k) -> p n k", n=8), axis=AX.X),
                     r=[kts[h][1]], w=[kmb])
            for h in range(4):
                d0, d1 = DR[h]
                ts(kmh[d0:d1, h, :], km[d0:d1, h, :], 1.0 / 256, None, ALU.mult, None, r=[kmb], w=[kmhb])
            wt, wtb = pre_v
            v_tm(wt, wtb, 0, 4, vt, vtb, 0)
            qsets = [[sb(es, f"mq{s_}{h}", [128, 512], BF16) for h in range(4)] for s_ in range(2)]
            gt, gtb = sb(es, "gate", [128, 4, 4, 8], F32)
            g8, g8b = sb(es, "g8", [128, 16, 8], F32)
            sst, sstb = sb(es, "selst", [128, 4, 4, 96], BF16)
            memset(sst[:], 0.0, w=[sstb])
            ost, ostb = sb(es, "ost", [128, 4, 4, 65], F32)
            rc, rcb = sb(es, "rc", [128, 4, 4], F32)
            obs = [sb(es, f"ob{i}", [128, 4, 256], BF16) for i in range(2)]
            oT, oTb = sb(es, "oT", [128, 2, 512], BF16)

            def P_units(c):
                qts = qsets[c % 2]
                return pairs_units_flat(l, [(wq, wqb, pr * 128, c * 512, 512, HG["qn_a"], qts[2 * pr], qts[2 * pr + 1], 0) for pr in range(2)])

            def SEL_units(c):
                qts = qsets[c % 2]

                def u_gates():
                    gtv0 = gt[:].rearrange("p j (hp two) n -> p j hp two n", two=2)
                    for par in range(2):
                        pg, pgb = RG['aux'].next()
                        first = True
                        for j in range(4):
                            for hp in range(2):
                                h = 2 * hp + par
                                mm(pg[:, (j * 2 + hp) * 8:(j * 2 + hp) * 8 + 8], qts[h][0][DR[h][0]:DR[h][1], j * 128:(j + 1) * 128],
                                   kmh[DR[h][0]:DR[h][1], h, :], first, False, r=[qts[h][1], kmhb], w=[pgb])
                                first = False
                        tt(gtv0[:, :, :, par, :], pg[:, 0:64].rearrange("p (j hp n) -> p j hp n", j=4, hp=2),
                           CT["mb_T"][:, 4 * c:4 * c + 4, :].unsqueeze(2).to_broadcast([128, 4, 2, 8]), ALU.add,
                           r=[pgb] + cb, w=[gtb])

                def u_topk():
                    for j in range(4):
                        for h in range(4):
                            P.op("dve", lambda e, j=j, h=h: e.max(out=g8[:, j * 4 + h, :], in_=gt[:, j, h, :]), r=[gtb], w=[g8b])
                    tt(gt[:], gt[:], g8[:, :, 2:3].rearrange("p (j h) o -> p j h o", j=4).to_broadcast([128, 4, 4, 8]), ALU.is_ge,
                       r=[gtb, g8b], w=[gtb])
                    tt(gt[:], gt[:], CT["mb_own"][:, 4 * c:4 * c + 4, :].unsqueeze(2).to_broadcast([128, 4, 4, 8]), ALU.max,
                       r=[gtb] + cb, w=[gtb])
                    gtv = gt[:].rearrange("p j (hp two) n -> p j hp two n", two=2)
                    sstv = sst[:].rearrange("p j (hp two) n -> p j hp two n", two=2)
                    ts(sstv[:, :, :, 0, 64:72], gtv[:, :, :, 0, :], BIG, -BIG, ALU.mult, ALU.add, r=[gtb], w=[sstb])
                    ts(sstv[:, :, :, 1, 0:8], gtv[:, :, :, 1, :], BIG, -BIG, ALU.mult, ALU.add, r=[gtb], w=[sstb])

                def u_tr(h):
                    def u():
                        pa, pab = RG['aux'].next()
                        pav = pa[:].bitcast(BF16)
                        for j in range(4):
                            tr(pav[0:96, j * 128:(j + 1) * 128], sst[:, j, h, :], ident_b[:], r=[sstb] + cb, w=[pab])
                        if h % 2 == 0:
                            cp(qts[h][0][64:72, :], pav[64:72, 0:512], r=[pab], w=[qts[h][1]])
                        else:
                            cp(qts[h][0][0:64, :], pav[0:64, 0:512], r=[pab], w=[qts[h][1]])
                    return u
                return [u_gates, u_topk] + [u_tr(h) for h in range(4)]

            def SEL(c):
                for u in SEL_units(c):
                    u()

            def A_units(c):
                qts = qsets[c % 2]
                return attn_units([dict(q=qts[h], k=kts[h], rows=ROWS[h], vt=vt, vtb=vtb, vslot=h, tiles=causal_tiles,
                                        evac=(lambda pv, pvb, h=h: cp(ost[:, :, h, :], pv[:, 0:260].rearrange("p (j d) -> p j d", j=4),
                                                                      r=[pvb], w=[ostb]))) for h in range(4)], c)

            def post(c):
                ob, obb = obs[c % 2]
                recip(rc[:], ost[:, :, :, 64], r=[ostb], w=[rcb])
                tt(ob[:].rearrange("p j (h d) -> p j h d", h=4), ost[:, :, :, 0:64],
                   rc[:].unsqueeze(3).to_broadcast([128, 4, 4, 64]), ALU.mult, r=[ostb, rcb], w=[obb])

            def D_units(c):
                ob, obb = obs[c % 2]
                return outproj_units(l, c, ob, obb, 2, wo, wob, oT, oTb)

            rings_alt(True)
            for u in P_units(0):
                u()
            SEL(0)
            for c in range(4):
                su = SEL_units(c + 1) if c < 3 else []
                side = (P_units(c + 1) if c < 3 else []) + su[:2] + (D_units(c - 1) if c > 0 else []) + su[2:]
                run_interleaved(A_units(c), side)
                post(c)
            for u in D_units(3):
                u()
            rings_alt(False)

    def nsa(l):
        PH.append(('nsa', dict(P.cnt)))
        pre_kcb = load_wcols(l, C_KCB, 256)
        P.barrier()
        with ExitStack() as es:
            ksel = [sb(es, f"nks{h}", [128, S], BF16) for h in range(2)]
            kwin1 = sb(es, "nkw", [128, S], BF16)
            vt, vtb = sb(es, "nv", [128, NT, 4, 65], BF16)
            memset(vt[:, :, :, 64:65], 1.0, w=[vtb])
            kcT1 = sb(es, "kcT", [128, 128], BF16)
            vca, vcab = sb(es, "vca", [128, 2, 98], BF16)
            wq, wqb = sb(es, "wq", [128, 8, 384], BF16)
            wg, wgb = sb(es, "wg", [128, 8, 18], BF16)
            wo, wob = sb(es, "wo", [128, 3, D], BF16)

            def late_loads():
                for kv in range(2):
                    for g in range(3):
                        load_w(wq[:, :, g * 128 + kv * 64:g * 128 + kv * 64 + 64], wqb, wview_in(l, C_QB + kv * 192 + g * 64, 64))
                load_w(wg[:], wgb, wview_in(l, C_GB, 18))
                load_w(wo[:], wob, W["w_out"][l][256:256 + 3 * 128, :].rearrange("(cc p) n -> p cc n", p=128))
            dma("sp", ksel[0][0][64:96, :], CD["ka_sel"], r=[], w=[ksel[0][1]])
            dma("sp", ksel[1][0][0:64, :], CD["kb_sel"], r=[], w=[ksel[1][1]])
            for h in range(2):
                dma("sp", vca[:, h, 64:97], CD["ovl"], r=[], w=[vcab])
            with ExitStack() as e1:
                kcb, kcbb = sb(e1, "kcb", [128, S], BF16)
                vcb, vcbb = sb(e1, "vcb", [128, S], BF16)
                w1 = [sb(e1, f"w1{k}", [128, 32, 128], BF16) for k in range(2)]
                w2 = [sb(e1, f"w2{k}", [128, 64], BF16) for k in range(2)]
                w2p = [sb(e1, f"w2p{k}", [128, 128], BF16) for k in range(2)]
                for k in range(2):
                    memset(w2p[k][0][:], 0.0, w=[w2p[k][1]])
                    load_w(w2p[k][0][:, k * 64:(k + 1) * 64], w2p[k][1], W["cmp_k_w2"][l])
                pe_t = [sb(e1, f"pe{k}", [32, 64], F32) for k in range(2)]
                peT = [sb(e1, f"peT{k}", [128, 32], BF16) for k in range(2)]
                hbias = [sb(e1, f"hb{k}", [128, 1], F32) for k in range(2)]
                for k, (n1, n2, npe) in enumerate((("cmp_k_w1", "cmp_k_w2", "cmp_pe_k"), ("cmp_v_w1", "cmp_v_w2", "cmp_pe_v"))):
                    src = W[n1][l].rearrange("(ll d) h -> d ll h", d=64)
                    load_w(w1[k][0][0:64], w1[k][1], src)
                    load_w(w1[k][0][64:128], w1[k][1], src)
                    load_w(w2[k][0][:], w2[k][1], W[n2][l])
                    dma("sp", pe_t[k][0][:], W[npe][l], r=[], w=[pe_t[k][1]])
                    px, pxb = RG['aux'].next()
                    tr(px[0:64, 0:32], pe_t[k][0][:], ident_f[0:32, 0:32], r=[pe_t[k][1]] + cb, w=[pxb])
                    cp(peT[k][0][0:64, :], px[0:64, 0:32], r=[pxb], w=[peT[k][1]])
                    py, pyb = RG['aux'].next()
                    for ll in range(32):
                        mm(py[:, 0:1], w1[k][0][0:64, ll, :], peT[k][0][0:64, ll:ll + 1], ll == 0, ll == 31,
                           r=[w1[k][1], peT[k][1]], w=[pyb])
                    cp(hbias[k][0][:], py[:, 0:1], r=[pyb], w=[hbias[k][1]])
                late_loads()
                wt, wtb = pre_kcb
                for k, (dst, dstb) in enumerate(((kcb, kcbb), (vcb, vcbb))):
                    for tc_ in range(4):
                        pa, pab = RG['proj'].next()
                        for kc in range(8):
                            mm(pa[:, :], wt[:, kc, k * 128:(k + 1) * 128], HT['t'][:, kc, tc_ * 512:(tc_ + 1) * 512], kc == 0, kc == 7,
                               r=[wtb, hb[tc_]], w=[pab])
                        act(dst[:, tc_ * 512:(tc_ + 1) * 512], pa[:, :], AF.Copy, r=[pab], w=[dstb])
                for k, (src, srcb) in enumerate(((kcb, kcbb), (vcb, vcbb))):
                    sv = src[:].rearrange("p (m r) -> p r m", r=16)
                    for h in range(2):
                        ph, phb = RG['aux'].next()
                        for ll in range(32):
                            r_, m0 = ll % 16, ll // 16
                            mm(ph[:, 0:127], w1[k][0][h * 64:(h + 1) * 64, ll, :], sv[h * 64:(h + 1) * 64, r_, m0:m0 + 127],
                               ll == 0, ll == 31, r=[w1[k][1], srcb], w=[phb])
                        u, ub = f32_r.next()
                        act(u[:, 0:127], ph[:, 0:127], AF.Identity, r=[phb, hbias[k][1]], w=[ub], bias=hbias[k][0][:, 0:1])
                        v2, v2b = f32_r.next()
                        tt(v2[:, 0:127], u[:, 0:127], u[:, 0:127], ALU.mult, r=[ub], w=[v2b])
                        ts(v2[:, 0:127], v2[:, 0:127], 0.044715, 1.0, ALU.mult, ALU.add, r=[v2b], w=[v2b])
                        tt(v2[:, 0:127], v2[:, 0:127], u[:, 0:127], ALU.mult, r=[v2b, ub], w=[v2b])
                        act(v2[:, 0:127], v2[:, 0:127], AF.Exp, r=[v2b], w=[v2b], scale=-1.5957691216057308)
                        ts(v2[:, 0:127], v2[:, 0:127], 1.0, None, ALU.add, None, r=[v2b], w=[v2b])
                        recip(v2[:, 0:127], v2[:, 0:127], r=[v2b], w=[v2b])
                        hd, hdb = pt_r.next()
                        tt(hd[:, 0:127], v2[:, 0:127], u[:, 0:127], ALU.mult, r=[v2b, ub], w=[hdb])
                        if k == 0:
                            if h == 0:
                                hd0, hd0b = hd, hdb
                            else:
                                pa, pab = RG['proj'].next()
                                mm(pa[:, 0:127], w2p[0][0][:, :], hd0[:, 0:127], True, False, r=[w2p[0][1], hd0b], w=[pab])
                                mm(pa[:, 0:127], w2p[1][0][:, :], hd[:, 0:127], False, True, r=[w2p[1][1], hdb], w=[pab])
                                finish_pair(l, pa, pab, 127, HG["kn_cmp"], kcT1[0], kcT1[1], kcT1[0], kcT1[1], 0, 0,
                                            tabC=CT["ropeC"][:].rearrange("p (m r) -> p r m", r=16)[:, 15, 1:128],
                                            tabS=CT["ropeS"][:].rearrange("p (m r) -> p r m", r=16)[:, 15, 1:128])
                        else:
                            pa, pab = RG['proj'].next()
                            mm(pa[0:127, 0:64], hd[:, 0:127], w2[1][0][:, :], True, True, r=[w2[1][1], hdb], w=[pab])
                            cp(vca[0:127, h, 0:64], pa[0:127, 0:64], r=[pab], w=[vcab])
                wt, wtb = load_wcols(l, C_KSB, 256)
                pairs_run(l, [(wt, wtb, 0, tc_ * 512, 512, HG["kn_sel"], ksel[0], ksel[1], tc_ * 512) for tc_ in range(4)])
                v_tm(wt, wtb, 128, 2, vt, vtb, 0)
                wt, wtb = load_wcols(l, C_KWB, 256)
                pairs_run(l, [(wt, wtb, 0, tc_ * 512, 512, HG["kn_win"], kwin1, kwin1, tc_ * 512) for tc_ in range(4)])
                v_tm(wt, wtb, 128, 2, vt, vtb, 2)
            PH.append(('nsa_q', dict(P.cnt)))
            P.barrier()
            qsets = [[sb(es, f"nq{s_}{h}", [128, 512], BF16) for h in range(6)] for s_ in range(2)]
            SELR = [(0, 96)] * 3 + [(0, 128)] * 3
            DATR = [(0, 64)] * 3 + [(64, 128)] * 3
            glog, glb = sb(es, "glog", [128, 4, 18], F32)
            ost, ostb = sb(es, "ost", [128, 4, 6, 65], F32)
            acc, accb = sb(es, "acc", [128, 4, 6, 64], F32)
            wts, wtsb = sb(es, "wts", [128, 4, 6], F32)
            imp, impb = sb(es, "imp", [128, 4, 2, 32], F32)
            i8, i8b = sb(es, "i8", [128, 8, 8], F32)
            sst, sstb = sb(es, "selst", [128, 4, 2, 96], BF16)
            memset(sst[:], 0.0, w=[sstb])
            obs = [sb(es, f"ob{i}", [128, 4, 384], BF16) for i in range(2)]
            oT, oTb = sb(es, "oT", [128, 3, 512], BF16)
            win_tiles = band_tiles(4, "m_lt")
            gv = glog[:].rearrange("p j (h b) -> p j h b", b=3)

            def P_units(c):
                qts = qsets[c % 2]
                return pairs_units_flat(l, [(wq, wqb, g * 128, c * 512, 512, HG["qn_b"], qts[g], qts[3 + g], 0) for g in range(3)])

            def cmp_tiles(cc):
                return [(0, 0, 4, [(jj, CT["m_cmp"][0:127, cc, jj * 128:(jj + 1) * 128]) for jj in range(4)])]

            def cmp_evac(h):
                kv, g = h // 3, h % 3

                def f(pv, pvb):
                    pvv = pv[:, 0:388].rearrange("p (j d) -> p j d", j=4)
                    cp(ost[:, :, h, :], pvv[:, :, 0:65], r=[pvb], w=[ostb])
                    ts(wts[:, :, h:h + 1], pvv[:, :, 64:65], 1e-30, None, ALU.max, None, r=[pvb], w=[wtsb])
                    recip(wts[:, :, h:h + 1], wts[:, :, h:h + 1], r=[wtsb], w=[wtsb])
                    if g == 0:
                        tt(imp[:, :, kv, :], pvv[:, :, 65:97], wts[:, :, h:h + 1].to_broadcast([128, 4, 32]), ALU.mult,
                           r=[pvb, wtsb], w=[impb])
                    else:
                        t1, t1b = f32_r.next()
                        t1v = t1[:, 0:128].rearrange("p (j n) -> p j n", j=4)
                        tt(t1v, pvv[:, :, 65:97], wts[:, :, h:h + 1].to_broadcast([128, 4, 32]), ALU.mult,
                           r=[pvb, wtsb], w=[t1b])
                        tt(imp[:, :, kv, :], imp[:, :, kv, :], t1v, ALU.add, r=[t1b, impb], w=[impb])
                return f

            def SELN(c):
                qts = qsets[c % 2]
                for j in range(4):
                    pa, pab = RG['proj'].next()
                    i = 4 * c + j
                    for kc in range(8):
                        mm(pa[:, 0:18], HT['t'][:, kc, i * 128:(i + 1) * 128], wg[:, kc, :], kc == 0, kc == 7, r=[wgb, hb[c]], w=[pab])
                    act(glog[:, j, :], pa[:, 0:18], AF.Exp, r=[pab], w=[glb], scale=-1.0)
                ts(glog[:], glog[:], 1.0, None, ALU.add, None, r=[glb], w=[glb])
                recip(glog[:], glog[:], r=[glb], w=[glb])
                attn_multi([dict(q=qts[h], k=kcT1, rows=DATR[h], vt_ap=vca[0:127, h // 3, 0:97], vtb=vcab, tiles=cmp_tiles,
                                 kparts=127, vcols=97, pvstride=97, evac=cmp_evac(h)) for h in range(6)], c)
                tt(imp[:], imp[:], CT["ns_A"][:, 4 * c:4 * c + 4, :].unsqueeze(2).to_broadcast([128, 4, 2, 32]), ALU.mult,
                   r=[impb] + cb, w=[impb])
                tt(imp[:], imp[:], CT["ns_B"][:, 4 * c:4 * c + 4, :].unsqueeze(2).to_broadcast([128, 4, 2, 32]), ALU.add,
                   r=[impb] + cb, w=[impb])
                for j in range(4):
                    for kv in range(2):
                        P.op("dve", lambda e, j=j, kv=kv: e.max(out=i8[:, j * 2 + kv, :], in_=imp[:, j, kv, :]), r=[impb], w=[i8b])
                tt(imp[:], imp[:], i8[:, :, 5:6].rearrange("p (j k) o -> p j k o", j=4).to_broadcast([128, 4, 2, 32]), ALU.is_ge,
                   r=[impb, i8b], w=[impb])
                ts(sst[:, :, 0, 64:96], imp[:, :, 0, :], BIG, -BIG, ALU.mult, ALU.add, r=[impb], w=[sstb])
                ts(sst[:, :, 1, 0:32], imp[:, :, 1, :], BIG, -BIG, ALU.mult, ALU.add, r=[impb], w=[sstb])
                for kv in range(2):
                    pa, pab = RG['aux'].next()
                    pav = pa[:].bitcast(BF16)
                    for j in range(4):
                        tr(pav[0:96, j * 128:(j + 1) * 128], sst[:, j, kv, :], ident_b[:], r=[sstb] + cb, w=[pab])
                    a0_, a1_ = (64, 96) if kv == 0 else (0, 64)
                    for g in range(3):
                        cp(qts[kv * 3 + g][0][a0_:a1_, :], pav[a0_:a1_, 0:512], r=[pab], w=[qts[kv * 3 + g][1]])
                tt(wts[:], wts[:], gv[:, :, :, 0], ALU.mult, r=[wtsb, glb], w=[wtsb])
                tt(acc[:], ost[:, :, :, 0:64], wts[:].unsqueeze(3).to_broadcast([128, 4, 6, 64]), ALU.mult, r=[ostb, wtsb], w=[accb])

            def A_units(c):
                qts = qsets[c % 2]
                ob, obb = obs[c % 2]
                units = []
                for br, (tiles_fn, RW, ks, slot0) in enumerate(((causal_tiles, SELR, ksel, 0), (win_tiles, DATR, [kwin1, kwin1], 2))):
                    units += attn_units([dict(q=qts[h], k=ks[h // 3], rows=RW[h], vt=vt, vtb=vtb, vslot=slot0 + h // 3, tiles=tiles_fn,
                                              evac=(lambda pv, pvb, h=h: cp(ost[:, :, h, 0:65], pv[:, 0:260].rearrange("p (j d) -> p j d", j=4),
                                                                            r=[pvb], w=[ostb]))) for h in range(6)], c)

                    def epi(br=br):
                        recip(wts[:], ost[:, :, :, 64], r=[ostb], w=[wtsb])
                        tt(wts[:], wts[:], gv[:, :, :, 1 + br], ALU.mult, r=[wtsb, glb], w=[wtsb])
                        tt(ost[:, :, :, 0:64], ost[:, :, :, 0:64], wts[:].unsqueeze(3).to_broadcast([128, 4, 6, 64]), ALU.mult,
                           r=[ostb, wtsb], w=[ostb])
                        if br == 0:
                            tt(acc[:], acc[:], ost[:, :, :, 0:64], ALU.add, r=[ostb, accb], w=[accb])
                        else:
                            tt(ob[:].rearrange("p j (h d) -> p j h d", h=6), acc[:], ost[:, :, :, 0:64], ALU.add, r=[ostb, accb], w=[obb])
                    units.append(epi)
                return units

            def D_units(c):
                ob, obb = obs[c % 2]
                return outproj_units(l, c, ob, obb, 3, wo, wob, oT, oTb)

            rings_alt(True)
            for u in P_units(0):
                u()
            SELN(0)
            for c in range(4):
                side = (D_units(c - 1) if c > 0 else []) + (P_units(c + 1) if c < 3 else [])
                run_interleaved(A_units(c), side)
                if c < 3:
                    SELN(c + 1)
            for u in D_units(3):
                u()
            rings_alt(False)

    def dilated(l):
        PH.append(('dil', dict(P.cnt)))
        pre_blk = [load_wcols(l, C_KC, 256), load_wcols(l, C_KC + 256, 128)]
        P.barrier()
        with ExitStack() as es:
            k01 = sb(es, "dk01", [128, S], BF16)
            kts = [k01, k01] + [sb(es, f"dk{h}", [128, S], BF16) for h in range(2, 6)]
            DROWS = [(0, 64), (64, 128), (0, 69), (0, 128), (0, 81), (0, 128)]
            vt, vtb = sb(es, "dv", [128, NT, 6, 65], BF16)
            memset(vt[:, :, :, 64:65], 1.0, w=[vtb])
            wq, wqb = sb(es, "wq", [128, 8, 384], BF16)
            load_w(wq[:], wqb, wview_in(l, C_QC, 384))
            wo, wob = load_wo(es, l, 640, 3)
            qsets = []
            for s_ in range(2):
                q01 = sb(es, f"dq01{s_}", [128, 512], BF16)
                qsets.append([q01, q01] + [sb(es, f"dq{s_}{h}", [128, 512], BF16) for h in range(2, 6)])
            for h, nm, dil in ((2, "d4", 4), (4, "d16", 16)):
                dma("sp", kts[h][0][64:65 + dil, :], CD["ka_" + nm], r=[], w=[kts[h][1]])
                dma("sp", kts[h + 1][0][0:64, :], CD["kb_" + nm], r=[], w=[kts[h + 1][1]])
                for s_ in range(2):
                    dma("sp", qsets[s_][h][0][64:65 + dil, :], CD["qa_" + nm], r=[], w=[qsets[s_][h][1]])
                    dma("sp", qsets[s_][h + 1][0][0:64, :], CD["qb_" + nm], r=[], w=[qsets[s_][h + 1][1]])
            for blk in range(2):
                wt, wtb = pre_blk[blk]
                pairs_run(l, [(wt, wtb, pp * 128, tc_ * 512, 512, HG["kn_c"], kts[blk * 4 + 2 * pp], kts[blk * 4 + 2 * pp + 1], tc_ * 512)
                              for pp in range(2 if blk == 0 else 1) for tc_ in range(4)])
            for blk in range(2):
                wt, wtb = load_wcols(l, C_VC + blk * 256, 256 if blk == 0 else 128)
                v_tm(wt, wtb, 0, 4 if blk == 0 else 2, vt, vtb, blk * 4)
            ost, ostb = sb(es, "ost", [128, 4, 6, 65], F32)
            dsum, dsb = sb(es, "dsum", [128, 4, 2], F32)
            obs = [sb(es, f"ob{i}", [128, 4, 384], BF16) for i in range(2)]
            oT, oTb = sb(es, "oT", [128, 3, 512], BF16)
            cfg = [(band_tiles(1, "m_le"), 64), (band_tiles(4, "m_le"), 69), (causal_tiles, 81)]

            def P_units(c):
                qts = qsets[c % 2]
                return pairs_units_flat(l, [(wq, wqb, pp * 128, c * 512, 512, HG["qn_c"], qts[2 * pp], qts[2 * pp + 1], 0) for pp in range(3)])

            def A_units(c):
                qts = qsets[c % 2]
                return attn_units([dict(q=qts[h], k=kts[h], rows=DROWS[h], vt=vt, vtb=vtb, vslot=h, tiles=cfg[h // 2][0],
                                        evac=(lambda pv, pvb, h=h: cp(ost[:, :, h, :], pv[:, 0:260].rearrange("p (j d) -> p j d", j=4),
                                                                      r=[pvb], w=[ostb]))) for h in range(6)], c)

            def post(c):
                ob, obb = obs[c % 2]
                tt(dsum[:], ost[:, :, 0:2, 64], ost[:, :, 2:4, 64], ALU.add, r=[ostb], w=[dsb])
                tt(dsum[:], dsum[:], ost[:, :, 4:6, 64], ALU.add, r=[ostb, dsb], w=[dsb])
                recip(dsum[:], dsum[:], r=[dsb], w=[dsb])
                obv = ob[:].rearrange("p j (g i d) -> p j g i d", g=3, i=2)
                for g in range(3):
                    tt(obv[:, :, g, :, :], ost[:, :, 2 * g:2 * g + 2, 0:64], dsum[:].unsqueeze(3).to_broadcast([128, 4, 2, 64]),
                       ALU.mult, r=[ostb, dsb], w=[obb])

            def D_units(c):
                ob, obb = obs[c % 2]
                return outproj_units(l, c, ob, obb, 3, wo, wob, oT, oTb)

            rings_alt(True)
            for u in P_units(0):
                u()
            for c in range(4):
                side = (P_units(c + 1) if c < 3 else []) + (D_units(c - 1) if c > 0 else [])
                run_interleaved(A_units(c), side)
                post(c)
            for u in D_units(3):
                u()
            rings_alt(False)

    def ffn(l):
        PH.append(('ffn', dict(P.cnt)))
        wvg = W["w_gate"][l].rearrange("(kc p) n -> p kc n", p=128)
        wvu = W["w_up"][l].rearrange("(kc p) n -> p kc n", p=128)
        pre = []
        for wv in (wvg, wvu):
            t_, tb_ = wring.next()
            load_w(t_[:], tb_, wv[:, :, 0:256])
            pre.append((t_, tb_))
        P.barrier()
        with ExitStack() as es:
            wd, wdb = sb(es, "wd", [128, NFC, D], BF16)
            wdv = W["w_down"][l].rearrange("(fc p) n -> p fc n", p=128)
            wd_bufs = [P.buf(f"wd{i}") for i in range(NFC // 2)]
            cw = cw_all[:, l]
            cwb = cwallb
            hid, hidb = sb(es, "hid", [128, NFC, 512], BF16)
            carry, carb = sb(es, "carry", [128, NFC, 2], F32)
            memset(carry[:], 0.0, w=[carb])
            g_r = Ring([f32_items[i] + (P.buf(f"gbc{i}"),) for i in range(2)])
            pend = [None]
            a_r = Ring([f32_items[2 + i] for i in range(2)])
            hq_r = [sb(es, f"hq{i}", [128, 8, 512], BF16) for i in range(2)]
            fring = Ring([sb(es, f"fw{i}", [128, 8, 256], BF16) for i in range(4)] + list(wring.items))
            norm_T(l, 1, list(range(0, 4)), hq_r[0][0], lambda i: hq_r[0][1], lambda i: (i % 4) * 128)
            FH = 12
            hidh = [P.buf("hid_lo"), P.buf("hid_hi")]

            def down_part(q, f0, f1):
                for j in range(4):
                    i = 4 * q + j
                    for hf in range(2):
                        pa, pab = RG['aux'].next()
                        for fc in range(f0, f1):
                            mm(pa[:, :], hid[:, fc, j * 128:(j + 1) * 128], wd[:, fc, hf * 512:(hf + 1) * 512], fc == f0, fc == f1 - 1,
                               r=[hidh[fc // FH], wd_bufs[fc // 2]], w=[pab])
                        tt(xs[:, i, hf * 512:(hf + 1) * 512], xs[:, i, hf * 512:(hf + 1) * 512], pa[:, :], ALU.add,
                           r=[pab, xb[i]], w=[xb[i]])

            for q in range(4):
                hq, hqb = hq_r[q % 2]
                for blk in range(NFC // 2):
                    if q == 0 and blk == 0:
                        (wgt, wgtb), (wut, wutb) = pre
                    else:
                        wgt, wgtb = fring.next()
                        load_w(wgt[:], wgtb, wvg[:, :, blk * 256:(blk + 1) * 256])
                        wut, wutb = fring.next()
                        load_w(wut[:], wutb, wvu[:, :, blk * 256:(blk + 1) * 256])
                    if q == 0:
                        load_w(wd[:, 2 * blk:2 * blk + 2, :], wd_bufs[blk], wdv[:, 2 * blk:2 * blk + 2, :])
                    for s_ in range(2):
                        fc = blk * 2 + s_
                        pg, pgb = RG['proj'].next()
                        for kc in range(8):
                            mm(pg[:, :], wgt[:, kc, s_ * 128:(s_ + 1) * 128], hq[:, kc, :], kc == 0, kc == 7, r=[wgtb, hqb], w=[pgb])
                        pu, pub = RG['s'].next()
                        for kc in range(8):
                            mm(pu[:, :], wut[:, kc, s_ * 128:(s_ + 1) * 128], hq[:, kc, :], kc == 0, kc == 7, r=[wutb, hqb], w=[pub])
                        gb_, gbm, gbc = g_r.next()
                        cp(gb_[:, 0:2], carry[:, fc, :], r=[carb], w=[gbc])
                        act(gb_[:, 2:514], pg[:, :], AF.Copy, r=[pgb], w=[gbm])
                        a_, ab_ = a_r.next()
                        act(a_[:, 0:512], pg[:, :], AF.Identity, r=[pgb, cwb], w=[ab_], scale=cw[:, 2, fc:fc + 1], bias=cw[:, 3, fc:fc + 1])
                        cp(carry[:, fc, :], gb_[:, 512:514], r=[gbm], w=[carb])
                        stt(a_[:, 0:512], gb_[:, 1:513], cw[:, 1, fc:fc + 1], a_[:, 0:512], ALU.mult, ALU.add, r=[gbm, gbc, cwb, ab_], w=[ab_])
                        stt(a_[:, 0:512], gb_[:, 0:512], cw[:, 0, fc:fc + 1], a_[:, 0:512], ALU.mult, ALU.add, r=[gbm, gbc, cwb, ab_], w=[ab_])
                        act(a_[:, 0:512], a_[:, 0:512], AF.Silu, r=[ab_], w=[ab_])
                        if pend[0] is not None:
                            pfc, pa_, pab_, ppu, ppub = pend[0]
                            tt(hid[:, pfc, :], pa_[:, 0:512], ppu[:, :], ALU.mult, r=[pab_, ppub], w=[hidh[pfc // FH]])
                        pend[0] = (fc, a_, ab_, pu, pub)
                    if blk == FH // 2:
                        down_part(q, 0, FH)
                pfc, pa_, pab_, ppu, ppub = pend[0]
                tt(hid[:, pfc, :], pa_[:, 0:512], ppu[:, :], ALU.mult, r=[pab_, ppub], w=[hidh[pfc // FH]])
                pend[0] = None
                if q < 3:
                    hqn, hqnb = hq_r[(q + 1) % 2]
                    norm_T(l, 1, list(range(4 * q + 4, 4 * q + 8)), hqn, lambda i: hqnb, lambda i: (i % 4) * 128)
                down_part(q, FH, NFC)

    for b in range(NB):
        for i in range(NT):
            if b > 0 or i < 4:
                continue
            dma("sp", xs[:, i, :], x_d[b, i * 128:(i + 1) * 128, :], r=[], w=[xb[i]])
        for l in range(NL):
            PH.append(('norm', dict(P.cnt)))
            if not (b == 0 and l == 0):
                P.barrier(skip_queues=("sp",))
            with ExitStack() as att:
                HT['t'], _ = sb(att, "hT", [128, 8, S], BF16)
                norm_T(l, 0, list(range(NT)), HT['t'], lambda i: hb[i // 4], lambda i: i * 128)
                moba(l)
                nsa(l)
                dilated(l)
            ffn(l)
        for i in range(NT):
            dma("sp", y_d[b, i * 128:(i + 1) * 128, :], xs[:, i, :], r=[xb[i]], w=[ybufs[i]])
            if b + 1 < NB:
                dma("sp", xs[:, i, :], x_d[b + 1, i * 128:(i + 1) * 128, :], r=[], w=[xb[i]])
    PH.append(('end', dict(P.cnt)))
    _CACHE['phases'] = PH
    P.final_wait("sp", ybufs)
    P.barrier()
    P.replay()
    root.close()
    return nc, consts


_CACHE = {}


def kernel(**inputs):
    n = 8
    x = np.ascontiguousarray(np.asarray(inputs["x"], dtype=np.float32))
    if "prog" not in _CACHE:
        _CACHE["prog"] = build(2, 2)
    nc, consts = _CACHE["prog"]
    in_maps = []
    for c in range(n):
        m = {"x": x[2 * c:2 * c + 2]}
        for k, v in inputs.items():
            if k != "x":
                m[k] = np.ascontiguousarray(np.asarray(v, dtype=np.float32))
        for k, v in consts.items():
            m["c_" + k] = v
        in_maps.append(m)
    res = run_bass_kernel_spmd(nc, in_maps, core_ids=list(range(n)))
    return np.concatenate([r["y"] for r in res.results], axis=0).astype(np.float32)
```

```python
import numpy as np
import ml_dtypes
from contextlib import ExitStack
import concourse.bass as bass
import concourse.mybir as mybir
from concourse.bass_utils import run_bass_kernel_spmd

F32 = mybir.dt.float32
BF16 = mybir.dt.bfloat16
ALU = mybir.AluOpType
AF = mybir.ActivationFunctionType
AX = mybir.AxisListType

S = 2048
D = 1024
NT = 16
DFF = 2816
NFC = 22
DIN = 3090
SCALE = 0.125
EPS = 1e-6
BIG = 32768.0
C_QA, C_KA, C_VA = 0, 256, 512
C_QB, C_KCB, C_VCB, C_KSB, C_VSB, C_KWB, C_VWB, C_GB = 768, 1152, 1280, 1408, 1536, 1664, 1792, 1920
C_QC, C_KC, C_VC = 1938, 2322, 2706

EPOCH = 24000
NSLOT = 8


class Buf:
    __slots__ = ("name", "w", "r")

    def __init__(self, name):
        self.name = name
        self.w = None
        self.r = {}


class Prog:
    ENGS = ("pe", "act", "dve", "pool")
    DMAQ = ("sp", "pool", "act")

    def __init__(self, nc):
        self.nc = nc
        self.streams = {k: [] for k in ("pe", "act", "dve", "pool", "sp")}
        self.cnt = {k: 0 for k in self.ENGS}
        self.esems = {k: [] for k in self.ENGS}
        self.dcnt = {k: 0 for k in self.DMAQ}
        self.dsems = {k: [nc.alloc_semaphore(name=f"d_{k}_{i}") for i in range(NSLOT)] for k in self.DMAQ}
        self.clock = {k: {} for k in self.streams}
        self.nbuf = 0

    def buf(self, name=None):
        self.nbuf += 1
        return Buf(name or f"b{self.nbuf}")

    def _esem(self, eng, seq):
        ep = (seq - 1) // EPOCH
        while len(self.esems[eng]) <= ep:
            self.esems[eng].append(self.nc.alloc_semaphore(name=f"e_{eng}_{len(self.esems[eng])}"))
        return self.esems[eng][ep], seq - ep * EPOCH

    def _tok_wait(self, tok):
        if tok[0] == "e":
            sem, val = self._esem(tok[1], tok[2])
            ep = (tok[2] - 1) // EPOCH
            return ("e", tok[1], ep), val, sem, val
        _, q, idx = tok
        slot = idx % NSLOT
        val = 16 * (idx // NSLOT + 1)
        return ("d", q, slot), val, self.dsems[q][slot], val

    def _waits_for(self, stream, toks):
        waits = []
        clk = self.clock[stream]
        for t in toks:
            if t[0] == "e" and t[1] == "pe" and stream == "pe":
                continue
            key, lvl, sem, val = self._tok_wait(t)
            if t[0] == "e":
                done = False
                for (k2, l2) in list(clk.items()):
                    if k2[0] == "e" and k2[1] == t[1] and k2[2] > key[2] and l2 > 0:
                        done = True
                if done:
                    continue
            if clk.get(key, 0) >= lvl:
                continue
            clk[key] = lvl
            waits.append((sem, val))
        return waits

    def _collect(self, stream, reads, writes):
        toks = set()
        for b in reads:
            if b.w is not None:
                toks.add(b.w)
        for b in writes:
            if b.w is not None:
                toks.add(b.w)
            for t in b.r.values():
                toks.add(t)
        return self._waits_for(stream, toks)

    def _mark(self, stream, tok, reads, writes):
        for b in writes:
            b.w = tok
            b.r = {}
        rk = tok if tok[0] == "d" else stream
        for b in reads:
            if b not in writes:
                b.r[rk] = tok

    def op(self, eng, fn, r=(), w=()):
        r = [b for b in r if b is not None]
        w = [b for b in w if b is not None]
        waits = self._collect(eng, r, w)
        self.cnt[eng] += 1
        seq = self.cnt[eng]
        sem, _ = self._esem(eng, seq)
        self.streams[eng].append((fn, waits, sem, 1))
        self._mark(eng, ("e", eng, seq), r, w)

    def dma(self, q, fn, r=(), w=()):
        r = [b for b in r if b is not None]
        w = [b for b in w if b is not None]
        waits = self._collect(q, r, w)
        idx = self.dcnt[q]
        self.dcnt[q] += 1
        slot = idx % NSLOT
        if idx >= NSLOT:
            key = ("d", q, slot)
            lvl = 16 * (idx // NSLOT)
            if self.clock[q].get(key, 0) < lvl:
                self.clock[q][key] = lvl
                waits.append((self.dsems[q][slot], lvl))
        self.streams[q].append((fn, waits, self.dsems[q][slot], 16))
        self._mark(q, ("d", q, idx), r, w)

    def barrier(self):
        toks = []
        for e in self.ENGS:
            if self.cnt[e] > 0:
                toks.append(("e", e, self.cnt[e]))
        for q in self.DMAQ:
            n = self.dcnt[q]
            for idx in range(max(0, n - NSLOT), n):
                toks.append(("d", q, idx))
        for s in self.streams:
            tk = [t for t in toks if not (t[0] == "e" and t[1] == s and s == "pe")]
            waits = self._waits_for(s, tk)
            if waits:
                self.streams[s].append((None, waits, None, 0))

    def final_wait(self, stream, bufs):
        waits = self._collect(stream, bufs, [])
        self.streams[stream].append((None, waits, None, 0))

    def replay(self):
        nc = self.nc
        with nc.Block() as block:
            def run(engine, items):
                for fn, waits, sem, inc in items:
                    for s, v in waits:
                        engine.wait_ge(s, v)
                    if fn is not None:
                        fn(engine).then_inc(sem, inc)

            @block.tensor
            def _(e):
                run(e, self.streams["pe"])

            @block.scalar
            def _(e):
                run(e, self.streams["act"])

            @block.vector
            def _(e):
                run(e, self.streams["dve"])

            @block.gpsimd
            def _(e):
                run(e, self.streams["pool"])

            @block.sync
            def _(e):
                run(e, self.streams["sp"])


class Ring:
    def __init__(self, items):
        self.items = items
        self.i = 0

    def next(self):
        it = self.items[self.i % len(self.items)]
        self.i += 1
        return it


def _bf(a):
    return np.ascontiguousarray(a).astype(ml_dtypes.bfloat16)


def make_consts():
    c = {}
    c["ident_f"] = np.eye(32, dtype=np.float32)
    c["ident_b"] = _bf(np.eye(128))
    obd = np.zeros((128, 128), np.float32)
    obd[0:64, 0:64] = 1.0
    obd[64:128, 64:128] = 1.0
    c["onesBD"] = _bf(obd)
    R = np.zeros((64, 64), np.float32)
    for i in range(8):
        R[i + 8, i] = -1.0
        R[i, i + 8] = 1.0
    rbd = np.zeros((128, 128), np.float32)
    rbd[0:64, 0:64] = R
    rbd[64:128, 64:128] = R
    c["rotBD"] = _bf(rbd)
    half = 8
    inv = (500000.0 ** (-(np.arange(half, dtype=np.float32) * 2.0 / 16.0))).astype(np.float32)
    pos = np.arange(S, dtype=np.float32)
    ang = (pos[None, :] * inv[:, None]).astype(np.float32)
    C = np.ones((128, S), np.float32)
    Sn = np.zeros((128, S), np.float32)
    for o in (0, 64):
        C[o:o + 8] = np.cos(ang)
        C[o + 8:o + 16] = np.cos(ang)
        Sn[o:o + 8] = np.sin(ang)
        Sn[o + 8:o + 16] = np.sin(ang)
    c["ropeC"] = _bf(C)
    c["ropeS"] = _bf(Sn)
    b = np.arange(128)[:, None]
    a = np.arange(128)[None, :]
    c["m_causal"] = _bf(np.where(b <= a, 0.0, -BIG))
    c["m_le"] = _bf(np.where(a <= b, 0.0, -BIG))
    c["m_lt"] = _bf(np.where(a < b, 0.0, -BIG))
    n = np.arange(128)[:, None, None]
    cc = np.arange(4)[None, :, None]
    tl = np.arange(512)[None, None, :]
    c["m_cmp"] = _bf(np.where((n <= 126) & (16 * n + 31 <= 512 * cc + tl), 0.0, -BIG))
    k = np.arange(S)[None, :]
    c["ka_moba"] = _bf((k // 256 == np.arange(8)[:, None]).astype(np.float32))
    c["ka_sel"] = _bf((k // 64 == np.arange(32)[:, None]).astype(np.float32))
    for nm, dil in (("d4", 4), ("d16", 16)):
        ka = np.zeros((dil + 1, S), np.float32)
        ka[:dil] = (k % dil == np.arange(dil)[:, None])
        ka[dil] = -BIG
        qa = np.zeros((dil + 1, 512), np.float32)
        qa[:dil] = BIG * (np.arange(512)[None, :] % dil == np.arange(dil)[:, None])
        qa[dil] = 1.0
        c["ka_" + nm] = _bf(ka)
        c["qa_" + nm] = _bf(qa)
        kb = np.zeros((64, S), np.float32)
        kb[:dil + 1] = ka
        qb = np.zeros((64, 512), np.float32)
        qb[:dil + 1] = qa
        c["kb_" + nm] = _bf(kb)
        c["qb_" + nm] = _bf(qb)
    kbm = np.zeros((64, S), np.float32)
    kbm[:8] = (k // 256 == np.arange(8)[:, None])
    c["kb_moba"] = _bf(kbm)
    kbs = np.zeros((64, S), np.float32)
    kbs[:32] = (k // 64 == np.arange(32)[:, None])
    c["kb_sel"] = _bf(kbs)
    p = np.arange(128)[:, None, None]
    i = np.arange(16)[None, :, None]
    t = 128 * i + p
    nb = np.arange(8)[None, None, :]
    cur = t // 256
    c["mb_T"] = _bf(np.where(nb < cur, 0.0, -1e30))
    c["mb_own"] = _bf((nb == cur).astype(np.float32))
    j = np.arange(32)[None, None, :]
    cur = t // 64
    forced = (j == 0) | (j == cur) | (j == cur - 1)
    vis = (j <= cur)
    c["ns_A"] = _bf(((~forced) & vis).astype(np.float32))
    c["ns_B"] = _bf(np.where(vis, np.where(forced, 1e9, 0.0), -1e30))
    starts = np.arange(127) * 16
    jj = np.arange(32)
    ov = (starts[:, None] < (jj[None, :] + 1) * 64) & (starts[:, None] + 32 > jj[None, :] * 64)
    ova = np.zeros((128, 33), np.float32)
    ova[:127, 0] = 1.0
    ova[:127, 1:] = ov
    c["ovl"] = _bf(ova)
    return c


CONST_SPECS = None


def build(NB=2, NL=2, dbg=None):
    nc = bass.Bass("TRN2", target_bir_lowering=False)
    P = Prog(nc)
    consts = make_consts()

    def din(name, shape, dt=F32):
        return nc.dram_tensor(name, list(shape), dt, kind="ExternalInput").ap()

    x_d = din("x", [NB, S, D])
    y_d = nc.dram_tensor("y", [NB, S, D], F32, kind="ExternalOutput").ap()
    W = {}
    for nm, shp in (("ln1", [2, D]), ("w_in", [2, D, DIN]), ("qn_a", [2, 64]), ("kn_a", [2, 64]), ("qn_b", [2, 64]),
                    ("kn_b", [2, 3, 64]), ("cmp_pe_k", [2, 32, 64]), ("cmp_pe_v", [2, 32, 64]),
                    ("cmp_k_w1", [2, 2048, 128]), ("cmp_k_w2", [2, 128, 64]), ("cmp_v_w1", [2, 2048, 128]),
                    ("cmp_v_w2", [2, 128, 64]), ("qn_c", [2, 64]), ("kn_c", [2, 64]), ("w_out", [2, D, D]),
                    ("ln2", [2, D]), ("w_gate", [2, D, DFF]), ("w_up", [2, D, DFF]), ("conv_w", [2, 3, DFF]),
                    ("conv_b", [2, DFF]), ("w_down", [2, DFF, D])):
        W[nm] = din(nm, shp)
    CD = {}
    for nm, arr in consts.items():
        CD[nm] = din("c_" + nm, arr.shape, BF16 if arr.dtype == ml_dtypes.bfloat16 else F32)
    dbg_d = None
    if dbg is not None:
        dbg_d = nc.dram_tensor("dbg", list(dbg), F32, kind="ExternalOutput").ap()
    ybufs = [P.buf(f"y{i}") for i in range(NT)]

    root = ExitStack()

    uid = [0]

    def sb(es, name, shape, dt):
        uid[0] += 1
        t = es.enter_context(nc.sbuf_tensor(f"{name}_{uid[0]}", list(shape), dt))
        return t, P.buf(name)

    def psb(es, name, shape, dt):
        t = es.enter_context(nc.psum_tensor(name, list(shape), dt))
        return t, P.buf(name)

    def mm(out, lhsT, rhs, start, stop, r, w):
        P.op("pe", lambda e: e.matmul(out, lhsT=lhsT, rhs=rhs, start=start, stop=stop, skip_group_check=True), r=r, w=w)

    def tr(out, in_, ident, r, w):
        P.op("pe", lambda e: e.transpose(out=out, in_=in_, identity=ident), r=r, w=w)

    def act(out, in_, func, r, w, scale=1.0, bias=None, accum_out=None):
        kw = {}
        if bias is not None:
            kw["bias"] = bias
        if accum_out is not None:
            kw["accum_out"] = accum_out
        P.op("act", lambda e: e.activation(out=out, in_=in_, func=func, scale=scale, **kw), r=r, w=w)

    def tt(out, in0, in1, op, r, w):
        P.op("dve", lambda e: e.tensor_tensor(out=out, in0=in0, in1=in1, op=op), r=r, w=w)

    def ts(out, in0, s1, s2, op0, op1, r, w):
        if s2 is None:
            nm = {ALU.mult: "tensor_scalar_mul", ALU.add: "tensor_scalar_add", ALU.max: "tensor_scalar_max"}[op0]
            P.op("dve", lambda e: getattr(e, nm)(out=out, in0=in0, scalar1=s1), r=r, w=w)
        else:
            P.op("dve", lambda e: e.tensor_scalar(out=out, in0=in0, scalar1=s1, scalar2=s2, op0=op0, op1=op1), r=r, w=w)

    def stt(out, in0, scalar, in1, op0, op1, r, w):
        P.op("dve", lambda e: e.scalar_tensor_tensor(out=out, in0=in0, scalar=scalar, in1=in1, op0=op0, op1=op1), r=r, w=w)

    def cp(out, in_, r, w):
        P.op("dve", lambda e: e.tensor_copy(out=out, in_=in_), r=r, w=w)

    def recip(out, in_, r, w):
        P.op("dve", lambda e: e.reciprocal(out=out, in_=in_), r=r, w=w)

    def memset(ap, val, w):
        P.op("dve", lambda e: e.memset(ap, val), w=w)

    def dma(q, out, in_, r, w, slow=False):
        if slow:
            P.dma(q, lambda e: e.dma_start(out=out, in_=in_, allow_slow_non_contiguous=True), r=r, w=w)
        else:
            P.dma(q, lambda e: e.dma_start(out=out, in_=in_), r=r, w=w)

    xs, _ = sb(root, "xs", [128, NT, D], F32)
    xb = [P.buf(f"x{i}") for i in range(NT)]
    hb = [P.buf(f"h{i}") for i in range(4)]
    HT = {}
    for i in range(4):
        dma("sp", xs[:, i, :], x_d[0, i * 128:(i + 1) * 128, :], r=[], w=[xb[i]])
    CT = {}
    cbuf = P.buf("consts")
    ebuf = P.buf("early_consts")
    EARLY = ("ident_f", "ident_b")
    for nm, arr in consts.items():
        shp = list(arr.shape)
        if nm[:3] in ("ka_", "qa_", "kb_", "qb_"):
            continue
        CT[nm], _ = sb(root, "k_" + nm, shp, BF16 if arr.dtype == ml_dtypes.bfloat16 else F32)
    for nm in EARLY:
        dma("sp", CT[nm][:], CD[nm], r=[], w=[ebuf])

    def load_rest_consts():
        for nm in CT:
            if nm not in EARLY:
                dma("sp", CT[nm][:], CD[nm], r=[], w=[cbuf])
    cb = [cbuf, ebuf]
    wring = Ring([sb(root, f"wst{i}", [128, 8, 256], BF16) for i in range(2)])
    HG = {"qn_a": 0, "kn_a": 1, "qn_b": 2, "kn_cmp": 3, "kn_sel": 4, "kn_win": 5, "qn_c": 6, "kn_c": 7}

    ps_proj = Ring([psb(root, f"psA{i}", [128, 512], F32) for i in range(2)])
    ps_s = Ring([psb(root, f"psS{i}", [128, 512], F32) for i in range(2)])
    ps_pv = Ring([psb(root, f"psV{i}", [128, 512], F32) for i in range(2)])
    ps_aux = Ring([psb(root, f"psX{i}", [128, 512], F32) for i in range(2)])
    RG = {'proj': ps_proj, 's': ps_s, 'aux': ps_aux}
    ALT = {'proj': Ring([ps_proj.items[0], ps_proj.items[1]]), 'aux': Ring([ps_aux.items[0]]),
           's': Ring([ps_s.items[0], ps_s.items[1], ps_aux.items[1]])}

    def rings_alt(on):
        RG['proj'] = ALT['proj'] if on else ps_proj
        RG['s'] = ALT['s'] if on else ps_s
        RG['aux'] = ALT['aux'] if on else ps_aux

    tmp = ExitStack()
    sq_r = Ring([sb(root, f"sq{i}", [128, 512], BF16) for i in range(2)])
    qg_r = Ring([sb(root, f"qg{i}", [128, 512], BF16) for i in range(2)])
    f32_items = [sb(root, f"f32t{i}", [128, 516], F32) for i in range(4)]
    f32_r = Ring(f32_items)
    pt_r = Ring([sb(root, f"pt{i}", [128, 512], BF16) for i in range(3)])
    xn_r = Ring([sb(root, f"xn{i}", [128, D], BF16) for i in range(1)])
    junk, junkb = xn_r.items[0]
    ssq, ssqb = sb(root, "ssq", [128, NT], F32)
    rstd, rstdb = sb(root, "rstd", [128, NT], F32)

    ident_b = CT["ident_b"]
    ident_f = CT["ident_f"]

    gains, gbuf = sb(root, "gains", [128, 2, 2, 8], F32)
    gst, gstb = f32_items[0]
    for l in range(2):
        for k, nm in enumerate(("ln1", "ln2")):
            r0 = (l * 2 + k) * 8
            dma("sp", gst[r0:r0 + 8, 0:128], W[nm][l].rearrange("(c p) -> c p", p=128), r=[], w=[gstb])
    pa_, pab_ = RG['aux'].next()
    tr(pa_[:, 0:32], gst[0:32, 0:128], ident_f[0:32, 0:32], r=[gstb, ebuf], w=[pab_])
    cp(gains[:].rearrange("p l k c -> p (l k c)"), pa_[:, 0:32], r=[pab_], w=[gbuf])
    load_rest_consts()
    hg, hgbuf = sb(root, "hgains", [128, 2, 8], F32)
    hst, hstb = f32_items[1]
    for l in range(2):
        for k, nm in ((0, "qn_a"), (1, "kn_a"), (2, "qn_b"), (6, "qn_c"), (7, "kn_c")):
            dma("sp", hst[l * 8 + k:l * 8 + k + 1, 0:64], W[nm][l:l + 1, :], r=[], w=[hstb])
        dma("sp", hst[l * 8 + 3:l * 8 + 6, 0:64], W["kn_b"][l], r=[], w=[hstb])
    dma("sp", hst[0:16, 64:128], hst[0:16, 0:64], r=[hstb], w=[hstb])
    HGL = [False]

    def hg_once():
        if HGL[0]:
            return
        HGL[0] = True
        pa2, pab2 = RG['aux'].next()
        tr(pa2[:, 0:16], hst[0:16, 0:128], ident_f[0:16, 0:16], r=[hstb, ebuf], w=[pab2])
        cp(hg[:].rearrange("p l k -> p (l k)"), pa2[:, 0:16], r=[pab2], w=[hgbuf])
    cw_all_t, cwallb = sb(root, "cw_all", [128, 2, 4, NFC], F32)
    cw_all = cw_all_t[:]
    CWL = [False]

    def load_conv_once():
        if CWL[0]:
            return
        CWL[0] = True
        for l in range(2):
            for k in range(3):
                dma("sp", cw_all[:, l, k, :], W["conv_w"][l, k].rearrange("(fc p) -> p fc", p=128), r=[], w=[cwallb], slow=True)
            dma("sp", cw_all[:, l, 3, :], W["conv_b"][l].rearrange("(fc p) -> p fc", p=128), r=[], w=[cwallb], slow=True)


    def wview_in(l, c0, width):
        return W["w_in"][l].rearrange("(kc p) n -> p kc n", p=128)[:, :, c0:c0 + width]

    def load_w(dst, dbuf, src):
        dma("pool", dst, src, r=[], w=[dbuf])

    def norm_T(l, which, tiles, dst, dst_bufs_of_tile, dst_col_of_tile):
        t0, t1 = tiles[0], tiles[-1] + 1
        memset(ssq[:, t0:t1], 0.0, w=[ssqb])
        for i in tiles:
            act(junk[:], xs[:, i, :], AF.Square, r=[xb[i]], w=[junkb, ssqb], accum_out=ssq[:, i:i + 1])
        act(rstd[:, t0:t1], ssq[:, t0:t1], AF.Ln, r=[ssqb], w=[rstdb], scale=1.0 / D, bias=EPS)
        act(rstd[:, t0:t1], rstd[:, t0:t1], AF.Exp, r=[rstdb], w=[rstdb], scale=-0.5)
        for i in tiles:
            xn, xnb = xn_r.next()
            act(xn[:], xs[:, i, :], AF.Copy, r=[xb[i], rstdb], w=[xnb], scale=rstd[:, i:i + 1])
            pa, pab = RG['aux'].next()
            pav = pa[:].bitcast(BF16).rearrange("p (c t) -> p c t", c=8)
            for c in range(8):
                tr(pav[:, c, :], xn[:, c * 128:(c + 1) * 128], ident_b[:], r=[xnb, ebuf], w=[pab])
            col = dst_col_of_tile(i)
            tt(dst[:, :, col:col + 128], pav, gains[:, l, which, :].unsqueeze(2).to_broadcast([128, 8, 128]), ALU.mult,
               r=[pab, gbuf], w=[dst_bufs_of_tile(i)])

    def finish_pair(l, pa, pab, ntok, gain_idx, dstA, dstAb, dstB, dstBb, dcol0, tok0, tabC=None, tabS=None):
        sq, sqb = sq_r.next()
        qg, qgb = qg_r.next()
        act(sq[:, 0:ntok], pa[:, 0:ntok], AF.Square, r=[pab], w=[sqb])
        act(qg[:, 0:ntok], pa[:, 0:ntok], AF.Copy, r=[pab, hgbuf], w=[qgb], scale=hg[:, l, gain_idx:gain_idx + 1])
        px, pxb = pa, pab
        mm(px[:, 0:ntok], CT["onesBD"][:], sq[:, 0:ntok], True, True, r=[sqb, qgb] + cb, w=[pxb])
        py, pyb = RG['aux'].next()
        mm(py[:, 0:ntok], CT["rotBD"][:], qg[:, 0:ntok], True, True, r=[qgb] + cb, w=[pyb])
        rs, rsb = f32_r.next()
        act(rs[:, 0:ntok], px[:, 0:ntok], AF.Ln, r=[pxb], w=[rsb], scale=1.0 / 64, bias=EPS)
        act(rs[:, 0:ntok], rs[:, 0:ntok], AF.Exp, r=[rsb], w=[rsb], scale=-0.5)
        if tabC is None:
            tabC = CT["ropeC"][:, tok0:tok0 + ntok]
            tabS = CT["ropeS"][:, tok0:tok0 + ntok]
        t1, t1b = f32_r.next()
        t2, t2b = f32_r.next()
        tt(t1[:, 0:ntok], qg[:, 0:ntok], tabC, ALU.mult, r=[qgb] + cb, w=[t1b])
        tt(t2[:, 0:ntok], py[:, 0:ntok], tabS, ALU.mult, r=[pyb] + cb, w=[t2b])
        tt(t1[:, 0:ntok], t1[:, 0:ntok], t2[:, 0:ntok], ALU.add, r=[t1b, t2b], w=[t1b])
        if dstA is dstB:
            tt(dstA[:, dcol0:dcol0 + ntok], t1[:, 0:ntok], rs[:, 0:ntok], ALU.mult, r=[t1b, rsb], w=[dstAb])
        else:
            tt(dstA[0:64, dcol0:dcol0 + ntok], t1[0:64, 0:ntok], rs[0:64, 0:ntok], ALU.mult, r=[t1b, rsb], w=[dstAb])
            tt(dstB[64:128, dcol0:dcol0 + ntok], t1[64:128, 0:ntok], rs[64:128, 0:ntok], ALU.mult, r=[t1b, rsb], w=[dstBb])

    def v_tm(wt, wtb, wcol, nh, vt, vtb, slot0):
        for i in range(NT):
            pa, pab = RG['proj'].next()
            for kc in range(8):
                mm(pa[:, 0:nh * 64], HT['t'][:, kc, i * 128:(i + 1) * 128], wt[:, kc, wcol:wcol + nh * 64], kc == 0, kc == 7,
                   r=[wtb, hb[i // 4]], w=[pab])
            act(vt[:, i, slot0:slot0 + nh, 0:64], pa[:, 0:nh * 64].rearrange("p (h d) -> p h d", h=nh), AF.Copy,
                r=[pab], w=[vtb])

    WARMN = [0]

    def attn_units(specs, c):
        items = []
        for sp in specs:
            lst = sp["tiles"](c)
            for n, (ki, a0, a1, masks) in enumerate(lst):
                items.append((sp, ki, a0, a1, masks, n == 0, n == len(lst) - 1))
        state = {}

        def qk(it):
            sp, ki, a0, a1, masks, isf, isl = it
            kparts = sp.get("kparts", 128)
            r0, r1 = sp["rows"]
            qt_, qtb = sp["q"]
            kt_, ktb = sp["k"]
            s_, sbf = RG['s'].next()
            nmask = len(masks)
            mm(s_[0:kparts, a0 * 128:a1 * 128], kt_[r0:r1, ki * 128:ki * 128 + kparts], qt_[r0:r1, a0 * 128:a1 * 128],
               True, nmask == 0, r=[ktb, qtb], w=[sbf])
            for mi, (j, mname) in enumerate(masks):
                msrc = CT[mname][0:kparts, :] if isinstance(mname, str) else mname
                mm(s_[0:kparts, j * 128:(j + 1) * 128], ident_b[0:kparts, 0:kparts], msrc, False, mi == nmask - 1,
                   r=cb, w=[sbf])
            return s_, sbf

        def rest(it, sres):
            sp, ki, a0, a1, masks, isf, isl = it
            s_, sbf = sres
            kparts = sp.get("kparts", 128)
            vcols = sp.get("vcols", 65)
            pvstride = sp.get("pvstride", 65)
            if isf:
                state["pv"] = ps_pv.next()
                state["started"] = False
            pv, pvb = state["pv"]
            pt, ptb = pt_r.next()
            act(pt[0:kparts, a0 * 128:a1 * 128], s_[0:kparts, a0 * 128:a1 * 128], AF.Exp, r=[sbf], w=[ptb], scale=SCALE)
            for j in range(a0, a1):
                rhs = sp["vt_ap"] if "vt_ap" in sp else sp["vt"][:, ki, sp["vslot"], 0:vcols]
                mm(pv[:, j * pvstride:j * pvstride + vcols], pt[0:kparts, j * 128:(j + 1) * 128], rhs,
                   not state["started"], False, r=[ptb, sp["vtb"]], w=[pvb])
                state["started"] = True
            if WARMN[0] > 0:
                jb, jbb = RG['proj'].next()
                mm(jb[:, 0:WARMN[0]], ident_b[:], pt[:, 0:WARMN[0]], True, True, r=[ptb] + cb, w=[jbb])
            if isl:
                sp["evac"](pv, pvb)

        units = []
        pend = []
        la = len(RG['s'].items) - 1

        def mk(it):
            def u():
                sres = qk(it)
                pend.append((it, sres))
                if len(pend) > la:
                    rest(*pend.pop(0))
            return u
        for it in items:
            units.append(mk(it))

        def last():
            while pend:
                rest(*pend.pop(0))
        units.append(last)
        return units

    def attn_multi(specs, c):
        for u in attn_units(specs, c):
            u()

    def run_interleaved(main, side):
        n, m = len(main), len(side)
        pos = {}
        for i in range(m):
            pos.setdefault(int((i + 0.5) * n / m), []).append(side[i])
        for k in range(n):
            for s_ in pos.get(k, []):
                s_()
            main[k]()
        for s_ in pos.get(n, []):
            s_()

    def pairs_units(l, jobs):
        box = [None]

        def mk(job):
            (wt, wtb, wcol, tok0, ntok, gain_idx, dA, dB, dcol0) = job

            def u():
                pa, pab = RG['proj'].next()
                for kc in range(8):
                    mm(pa[:, 0:ntok], wt[:, kc, wcol:wcol + 128], HT['t'][:, kc, tok0:tok0 + ntok], kc == 0, kc == 7,
                       r=[wtb, hb[tok0 // 512]], w=[pab])
                if box[0] is not None:
                    finish_pair(l, *box[0])
                box[0] = (pa, pab, ntok, gain_idx, dA[0], dA[1], dB[0], dB[1], dcol0, tok0)
            return u

        def last():
            if box[0] is not None:
                finish_pair(l, *box[0])
                box[0] = None
        return [mk(j) for j in jobs] + [last]

    def pairs_units_flat(l, jobs):
        units = []
        for job in jobs:
            (wt, wtb, wcol, tok0, ntok, gain_idx, dA, dB, dcol0) = job
            st = {}

            def pu(wt=wt, wtb=wtb, wcol=wcol, tok0=tok0, ntok=ntok, st=st):
                pa, pab = RG['proj'].next()
                for kc in range(8):
                    mm(pa[:, 0:ntok], wt[:, kc, wcol:wcol + 128], HT['t'][:, kc, tok0:tok0 + ntok], kc == 0, kc == 7,
                       r=[wtb, hb[tok0 // 512]], w=[pab])
                st['pa'] = (pa, pab)

            def fu(ntok=ntok, gain_idx=gain_idx, dA=dA, dB=dB, dcol0=dcol0, tok0=tok0, st=st):
                pa, pab = st['pa']
                finish_pair(l, pa, pab, ntok, gain_idx, dA[0], dA[1], dB[0], dB[1], dcol0, tok0)
            units += [pu, fu]
        return units

    def pairs_run(l, jobs):
        for u in pairs_units(l, jobs):
            u()

    def causal_tiles(c):
        out = []
        for ki in range(4 * c + 4):
            a0 = max(0, ki - 4 * c)
            masks = [(ki - 4 * c, "m_causal")] if ki >= 4 * c else []
            out.append((ki, a0, 4, masks))
        return out

    def band_tiles(maxd_tiles, far_mask, causal_only_class=False):
        def fn(c):
            out = []
            for ki in range(max(0, 4 * c - maxd_tiles), 4 * c + 4):
                a0 = max(0, ki - 4 * c)
                a1 = min(4, ki + maxd_tiles + 1 - 4 * c)
                if a1 <= a0:
                    continue
                masks = []
                if ki >= 4 * c:
                    masks.append((ki - 4 * c, "m_causal"))
                jf = ki + maxd_tiles - 4 * c
                if far_mask is not None and 0 <= jf < 4:
                    masks.append((jf, far_mask))
                out.append((ki, a0, a1, masks))
            return out
        return fn

    def outproj_units(l, c, ob, obb, ncc, wo, wob, oT, oTb):
        def tr_unit(j):
            def u():
                pa, pab = RG['aux'].next()
                pav = pa[:].bitcast(BF16).rearrange("p (c t) -> p c t", c=8)
                for cc in range(ncc):
                    tr(pav[:, cc, :], ob[:, j, cc * 128:(cc + 1) * 128], ident_b[:], r=[obb] + cb, w=[pab])
                cp(oT[:, 0:ncc, j * 128:(j + 1) * 128], pav[:, 0:ncc, :], r=[pab], w=[oTb])
            return u

        def mm_unit(j, hf):
            def u():
                i = 4 * c + j
                pa, pab = RG['proj'].next()
                for cc in range(ncc):
                    mm(pa[:, :], oT[:, cc, j * 128:(j + 1) * 128], wo[:, cc, hf * 512:(hf + 1) * 512], cc == 0, cc == ncc - 1,
                       r=[oTb, wob], w=[pab])
                tt(xs[:, i, hf * 512:(hf + 1) * 512], xs[:, i, hf * 512:(hf + 1) * 512], pa[:, :], ALU.add,
                   r=[pab, xb[i]], w=[xb[i]])
            return u
        return [tr_unit(j) for j in range(4)] + [mm_unit(j, hf) for j in range(4) for hf in range(2)]

    def out_proj(l, c, ob, obb, ncc, wo, wob, oT, oTb):
        for u in outproj_units(l, c, ob, obb, ncc, wo, wob, oT, oTb):
            u()

    def load_wo(es, l, row0, ncc):
        wo, wob = sb(es, "wo", [128, ncc, D], BF16)
        load_w(wo[:], wob, W["w_out"][l][row0:row0 + ncc * 128, :].rearrange("(cc p) n -> p cc n", p=128))
        return wo, wob

    def load_wcols(l, c0, width):
        wt, wtb = wring.next()
        load_w(wt[:, :, 0:width], wtb, wview_in(l, c0, width))
        return wt, wtb

    PH = []

    def moba(l):
        PH.append(('moba', dict(P.cnt)))
        pre_k = load_wcols(l, C_KA, 256)
        pre_v = load_wcols(l, C_VA, 256)
        with ExitStack() as es:
            kts = [sb(es, f"mk{h}", [128, S], BF16) for h in range(4)]
            ROWS = [(0, 72), (0, 128), (0, 72), (0, 128)]
            DR = [(0, 64), (64, 128), (0, 64), (64, 128)]
            vt, vtb = sb(es, "mv", [128, NT, 4, 65], BF16)
            memset(vt[:, :, :, 64:65], 1.0, w=[vtb])
            km, kmb = sb(es, "kmean", [128, 4, 8], F32)
            kmh, kmhb = sb(es, "kmeanb", [128, 4, 8], BF16)
            wq, wqb = sb(es, "wq", [128, 8, 256], BF16)
            load_w(wq[:], wqb, wview_in(l, C_QA, 256))
            wo, wob = load_wo(es, l, 0, 2)
            for h in (0, 2):
                dma("sp", kts[h][0][64:72, :], CD["ka_moba"], r=[], w=[kts[h][1]])
            for h in (1, 3):
                dma("sp", kts[h][0][0:64, :], CD["kb_moba"], r=[], w=[kts[h][1]])
            load_conv_once()
            hg_once()
            wt, wtb = pre_k
            pairs_run(l, [(wt, wtb, pr * 128, tc_ * 512, 512, HG["kn_a"], kts[2 * pr], kts[2 * pr + 1], tc_ * 512)
                          for pr in range(2) for tc_ in range(4)])
            for h in range(4):
                d0, d1 = DR[h]
                P.op("dve", lambda e, h=h, d0=d0, d1=d1: e.reduce_sum(out=km[d0:d1, h, :], in_=kts[h][0][d0:d1, :].rearrange("p (n k) -> p n k", n=8), axis=AX.X),
                     r=[kts[h][1]], w=[kmb])
            for h in range(4):
                d0, d1 = DR[h]
                ts(kmh[d0:d1, h, :], km[d0:d1, h, :], 1.0 / 256, None, ALU.mult, None, r=[kmb], w=[kmhb])
            wt, wtb = pre_v
            v_tm(wt, wtb, 0, 4, vt, vtb, 0)
            qsets = [[sb(es, f"mq{s_}{h}", [128, 512], BF16) for h in range(4)] for s_ in range(2)]
            gt, gtb = sb(es, "gate", [128, 4, 4, 8], F32)
            g8, g8b = sb(es, "g8", [128, 16, 8], F32)
            sst, sstb = sb(es, "selst", [128, 4, 4, 96], BF16)
            memset(sst[:], 0.0, w=[sstb])
            ost, ostb = sb(es, "ost", [128, 4, 4, 65], F32)
            rc, rcb = sb(es, "rc", [128, 4, 4], F32)
            obs = [sb(es, f"ob{i}", [128, 4, 256], BF16) for i in range(2)]
            oT, oTb = sb(es, "oT", [128, 2, 512], BF16)

            def P_units(c):
                qts = qsets[c % 2]
                return pairs_units(l, [(wq, wqb, pr * 128, c * 512, 512, HG["qn_a"], qts[2 * pr], qts[2 * pr + 1], 0) for pr in range(2)])

            def SEL_units(c):
                qts = qsets[c % 2]

                def u_gates():
                    gtv0 = gt[:].rearrange("p j (hp two) n -> p j hp two n", two=2)
                    for par in range(2):
                        pg, pgb = RG['aux'].next()
                        first = True
                        for j in range(4):
                            for hp in range(2):
                                h = 2 * hp + par
                                mm(pg[:, (j * 2 + hp) * 8:(j * 2 + hp) * 8 + 8], qts[h][0][DR[h][0]:DR[h][1], j * 128:(j + 1) * 128],
                                   kmh[DR[h][0]:DR[h][1], h, :], first, False, r=[qts[h][1], kmhb], w=[pgb])
                                first = False
                        tt(gtv0[:, :, :, par, :], pg[:, 0:64].rearrange("p (j hp n) -> p j hp n", j=4, hp=2),
                           CT["mb_T"][:, 4 * c:4 * c + 4, :].unsqueeze(2).to_broadcast([128, 4, 2, 8]), ALU.add,
                           r=[pgb] + cb, w=[gtb])

                def u_topk():
                    for j in range(4):
                        for h in range(4):
                            P.op("dve", lambda e, j=j, h=h: e.max(out=g8[:, j * 4 + h, :], in_=gt[:, j, h, :]), r=[gtb], w=[g8b])
                    tt(gt[:], gt[:], g8[:, :, 2:3].rearrange("p (j h) o -> p j h o", j=4).to_broadcast([128, 4, 4, 8]), ALU.is_ge,
                       r=[gtb, g8b], w=[gtb])
                    tt(gt[:], gt[:], CT["mb_own"][:, 4 * c:4 * c + 4, :].unsqueeze(2).to_broadcast([128, 4, 4, 8]), ALU.max,
                       r=[gtb] + cb, w=[gtb])
                    gtv = gt[:].rearrange("p j (hp two) n -> p j hp two n", two=2)
                    sstv = sst[:].rearrange("p j (hp two) n -> p j hp two n", two=2)
                    ts(sstv[:, :, :, 0, 64:72], gtv[:, :, :, 0, :], BIG, -BIG, ALU.mult, ALU.add, r=[gtb], w=[sstb])
                    ts(sstv[:, :, :, 1, 0:8], gtv[:, :, :, 1, :], BIG, -BIG, ALU.mult, ALU.add, r=[gtb], w=[sstb])

                def u_tr(h):
                    def u():
                        pa, pab = RG['aux'].next()
                        pav = pa[:].bitcast(BF16)
                        for j in range(4):
                            tr(pav[0:96, j * 128:(j + 1) * 128], sst[:, j, h, :], ident_b[:], r=[sstb] + cb, w=[pab])
                        if h % 2 == 0:
                            cp(qts[h][0][64:72, :], pav[64:72, 0:512], r=[pab], w=[qts[h][1]])
                        else:
                            cp(qts[h][0][0:64, :], pav[0:64, 0:512], r=[pab], w=[qts[h][1]])
                    return u
                return [u_gates, u_topk] + [u_tr(h) for h in range(4)]

            def SEL(c):
                for u in SEL_units(c):
                    u()

            def A_units(c):
                qts = qsets[c % 2]
                return attn_units([dict(q=qts[h], k=kts[h], rows=ROWS[h], vt=vt, vtb=vtb, vslot=h, tiles=causal_tiles,
                                        evac=(lambda pv, pvb, h=h: cp(ost[:, :, h, :], pv[:, 0:260].rearrange("p (j d) -> p j d", j=4),
                                                                      r=[pvb], w=[ostb]))) for h in range(4)], c)

            def post(c):
                ob, obb = obs[c % 2]
                recip(rc[:], ost[:, :, :, 64], r=[ostb], w=[rcb])
                tt(ob[:].rearrange("p j (h d) -> p j h d", h=4), ost[:, :, :, 0:64],
                   rc[:].unsqueeze(3).to_broadcast([128, 4, 4, 64]), ALU.mult, r=[ostb, rcb], w=[obb])

            def D_units(c):
                ob, obb = obs[c % 2]
                return outproj_units(l, c, ob, obb, 2, wo, wob, oT, oTb)

            rings_alt(True)
            for u in P_units(0):
                u()
            SEL(0)
            for c in range(4):
                su = SEL_units(c + 1) if c < 3 else []
                side = (P_units(c + 1) if c < 3 else []) + su[:2] + (D_units(c - 1) if c > 0 else []) + su[2:]
                run_interleaved(A_units(c), side)
                post(c)
            for u in D_units(3):
                u()
            rings_alt(False)

    def nsa(l):
        PH.append(('nsa', dict(P.cnt)))
        pre_kcb = load_wcols(l, C_KCB, 256)
        P.barrier()
        with ExitStack() as es:
            ksel = [sb(es, f"nks{h}", [128, S], BF16) for h in range(2)]
            kwin1 = sb(es, "nkw", [128, S], BF16)
            vt, vtb = sb(es, "nv", [128, NT, 4, 65], BF16)
            memset(vt[:, :, :, 64:65], 1.0, w=[vtb])
            kcT1 = sb(es, "kcT", [128, 128], BF16)
            vca, vcab = sb(es, "vca", [128, 2, 98], BF16)
            wq, wqb = sb(es, "wq", [128, 8, 384], BF16)
            wg, wgb = sb(es, "wg", [128, 8, 18], BF16)
            wo, wob = sb(es, "wo", [128, 3, D], BF16)

            def late_loads():
                for kv in range(2):
                    for g in range(3):
                        load_w(wq[:, :, g * 128 + kv * 64:g * 128 + kv * 64 + 64], wqb, wview_in(l, C_QB + kv * 192 + g * 64, 64))
                load_w(wg[:], wgb, wview_in(l, C_GB, 18))
                load_w(wo[:], wob, W["w_out"][l][256:256 + 3 * 128, :].rearrange("(cc p) n -> p cc n", p=128))
            dma("sp", ksel[0][0][64:96, :], CD["ka_sel"], r=[], w=[ksel[0][1]])
            dma("sp", ksel[1][0][0:64, :], CD["kb_sel"], r=[], w=[ksel[1][1]])
            for h in range(2):
                dma("sp", vca[:, h, 64:97], CD["ovl"], r=[], w=[vcab])
            with ExitStack() as e1:
                kcb, kcbb = sb(e1, "kcb", [128, S], BF16)
                vcb, vcbb = sb(e1, "vcb", [128, S], BF16)
                w1 = [sb(e1, f"w1{k}", [128, 32, 128], BF16) for k in range(2)]
                w2 = [sb(e1, f"w2{k}", [128, 64], BF16) for k in range(2)]
                w2p = [sb(e1, f"w2p{k}", [128, 128], BF16) for k in range(2)]
                for k in range(2):
                    memset(w2p[k][0][:], 0.0, w=[w2p[k][1]])
                    load_w(w2p[k][0][:, k * 64:(k + 1) * 64], w2p[k][1], W["cmp_k_w2"][l])
                pe_t = [sb(e1, f"pe{k}", [32, 64], F32) for k in range(2)]
                peT = [sb(e1, f"peT{k}", [128, 32], BF16) for k in range(2)]
                hbias = [sb(e1, f"hb{k}", [128, 1], F32) for k in range(2)]
                for k, (n1, n2, npe) in enumerate((("cmp_k_w1", "cmp_k_w2", "cmp_pe_k"), ("cmp_v_w1", "cmp_v_w2", "cmp_pe_v"))):
                    src = W[n1][l].rearrange("(ll d) h -> d ll h", d=64)
                    load_w(w1[k][0][0:64], w1[k][1], src)
                    load_w(w1[k][0][64:128], w1[k][1], src)
                    load_w(w2[k][0][:], w2[k][1], W[n2][l])
                    dma("sp", pe_t[k][0][:], W[npe][l], r=[], w=[pe_t[k][1]])
                    px, pxb = RG['aux'].next()
                    tr(px[0:64, 0:32], pe_t[k][0][:], ident_f[0:32, 0:32], r=[pe_t[k][1]] + cb, w=[pxb])
                    cp(peT[k][0][0:64, :], px[0:64, 0:32], r=[pxb], w=[peT[k][1]])
                    py, pyb = RG['aux'].next()
                    for ll in range(32):
                        mm(py[:, 0:1], w1[k][0][0:64, ll, :], peT[k][0][0:64, ll:ll + 1], ll == 0, ll == 31,
                           r=[w1[k][1], peT[k][1]], w=[pyb])
                    cp(hbias[k][0][:], py[:, 0:1], r=[pyb], w=[hbias[k][1]])
                late_loads()
                wt, wtb = pre_kcb
                for k, (dst, dstb) in enumerate(((kcb, kcbb), (vcb, vcbb))):
                    for tc_ in range(4):
                        pa, pab = RG['proj'].next()
                        for kc in range(8):
                            mm(pa[:, :], wt[:, kc, k * 128:(k + 1) * 128], HT['t'][:, kc, tc_ * 512:(tc_ + 1) * 512], kc == 0, kc == 7,
                               r=[wtb, hb[tc_]], w=[pab])
                        act(dst[:, tc_ * 512:(tc_ + 1) * 512], pa[:, :], AF.Copy, r=[pab], w=[dstb])
                for k, (src, srcb) in enumerate(((kcb, kcbb), (vcb, vcbb))):
                    sv = src[:].rearrange("p (m r) -> p r m", r=16)
                    for h in range(2):
                        ph, phb = RG['aux'].next()
                        for ll in range(32):
                            r_, m0 = ll % 16, ll // 16
                            mm(ph[:, 0:127], w1[k][0][h * 64:(h + 1) * 64, ll, :], sv[h * 64:(h + 1) * 64, r_, m0:m0 + 127],
                               ll == 0, ll == 31, r=[w1[k][1], srcb], w=[phb])
                        u, ub = f32_r.next()
                        act(u[:, 0:127], ph[:, 0:127], AF.Identity, r=[phb, hbias[k][1]], w=[ub], bias=hbias[k][0][:, 0:1])
                        v2, v2b = f32_r.next()
                        tt(v2[:, 0:127], u[:, 0:127], u[:, 0:127], ALU.mult, r=[ub], w=[v2b])
                        ts(v2[:, 0:127], v2[:, 0:127], 0.044715, 1.0, ALU.mult, ALU.add, r=[v2b], w=[v2b])
                        tt(v2[:, 0:127], v2[:, 0:127], u[:, 0:127], ALU.mult, r=[v2b, ub], w=[v2b])
                        act(v2[:, 0:127], v2[:, 0:127], AF.Exp, r=[v2b], w=[v2b], scale=-1.5957691216057308)
                        ts(v2[:, 0:127], v2[:, 0:127], 1.0, None, ALU.add, None, r=[v2b], w=[v2b])
                        recip(v2[:, 0:127], v2[:, 0:127], r=[v2b], w=[v2b])
                        hd, hdb = pt_r.next()
                        tt(hd[:, 0:127], v2[:, 0:127], u[:, 0:127], ALU.mult, r=[v2b, ub], w=[hdb])
                        if k == 0:
                            if h == 0:
                                hd0, hd0b = hd, hdb
                            else:
                                pa, pab = RG['proj'].next()
                                mm(pa[:, 0:127], w2p[0][0][:, :], hd0[:, 0:127], True, False, r=[w2p[0][1], hd0b], w=[pab])
                                mm(pa[:, 0:127], w2p[1][0][:, :], hd[:, 0:127], False, True, r=[w2p[1][1], hdb], w=[pab])
                                finish_pair(l, pa, pab, 127, HG["kn_cmp"], kcT1[0], kcT1[1], kcT1[0], kcT1[1], 0, 0,
                                            tabC=CT["ropeC"][:].rearrange("p (m r) -> p r m", r=16)[:, 15, 1:128],
                                            tabS=CT["ropeS"][:].rearrange("p (m r) -> p r m", r=16)[:, 15, 1:128])
                        else:
                            pa, pab = RG['proj'].next()
                            mm(pa[0:127, 0:64], hd[:, 0:127], w2[1][0][:, :], True, True, r=[w2[1][1], hdb], w=[pab])
                            cp(vca[0:127, h, 0:64], pa[0:127, 0:64], r=[pab], w=[vcab])
                wt, wtb = load_wcols(l, C_KSB, 256)
                pairs_run(l, [(wt, wtb, 0, tc_ * 512, 512, HG["kn_sel"], ksel[0], ksel[1], tc_ * 512) for tc_ in range(4)])
                v_tm(wt, wtb, 128, 2, vt, vtb, 0)
                wt, wtb = load_wcols(l, C_KWB, 256)
                pairs_run(l, [(wt, wtb, 0, tc_ * 512, 512, HG["kn_win"], kwin1, kwin1, tc_ * 512) for tc_ in range(4)])
                v_tm(wt, wtb, 128, 2, vt, vtb, 2)
            PH.append(('nsa_q', dict(P.cnt)))
            P.barrier()
            qsets = [[sb(es, f"nq{s_}{h}", [128, 512], BF16) for h in range(6)] for s_ in range(2)]
            SELR = [(0, 96)] * 3 + [(0, 128)] * 3
            DATR = [(0, 64)] * 3 + [(64, 128)] * 3
            glog, glb = sb(es, "glog", [128, 4, 18], F32)
            ost, ostb = sb(es, "ost", [128, 4, 6, 65], F32)
            acc, accb = sb(es, "acc", [128, 4, 6, 64], F32)
            wts, wtsb = sb(es, "wts", [128, 4, 6], F32)
            imp, impb = sb(es, "imp", [128, 4, 2, 32], F32)
            i8, i8b = sb(es, "i8", [128, 8, 8], F32)
            sst, sstb = sb(es, "selst", [128, 4, 2, 96], BF16)
            memset(sst[:], 0.0, w=[sstb])
            obs = [sb(es, f"ob{i}", [128, 4, 384], BF16) for i in range(2)]
            oT, oTb = sb(es, "oT", [128, 3, 512], BF16)
            win_tiles = band_tiles(4, "m_lt")
            gv = glog[:].rearrange("p j (h b) -> p j h b", b=3)

            def P_units(c):
                qts = qsets[c % 2]
                return pairs_units(l, [(wq, wqb, g * 128, c * 512, 512, HG["qn_b"], qts[g], qts[3 + g], 0) for g in range(3)])

            def cmp_tiles(cc):
                return [(0, 0, 4, [(jj, CT["m_cmp"][0:127, cc, jj * 128:(jj + 1) * 128]) for jj in range(4)])]

            def cmp_evac(h):
                kv, g = h // 3, h % 3

                def f(pv, pvb):
                    pvv = pv[:, 0:388].rearrange("p (j d) -> p j d", j=4)
                    cp(ost[:, :, h, :], pvv[:, :, 0:65], r=[pvb], w=[ostb])
                    ts(wts[:, :, h:h + 1], pvv[:, :, 64:65], 1e-30, None, ALU.max, None, r=[pvb], w=[wtsb])
                    recip(wts[:, :, h:h + 1], wts[:, :, h:h + 1], r=[wtsb], w=[wtsb])
                    if g == 0:
                        tt(imp[:, :, kv, :], pvv[:, :, 65:97], wts[:, :, h:h + 1].to_broadcast([128, 4, 32]), ALU.mult,
                           r=[pvb, wtsb], w=[impb])
                    else:
                        t1, t1b = f32_r.next()
                        t1v = t1[:, 0:128].rearrange("p (j n) -> p j n", j=4)
                        tt(t1v, pvv[:, :, 65:97], wts[:, :, h:h + 1].to_broadcast([128, 4, 32]), ALU.mult,
                           r=[pvb, wtsb], w=[t1b])
                        tt(imp[:, :, kv, :], imp[:, :, kv, :], t1v, ALU.add, r=[t1b, impb], w=[impb])
                return f

            def SELN(c):
                qts = qsets[c % 2]
                for j in range(4):
                    pa, pab = RG['proj'].next()
                    i = 4 * c + j
                    for kc in range(8):
                        mm(pa[:, 0:18], HT['t'][:, kc, i * 128:(i + 1) * 128], wg[:, kc, :], kc == 0, kc == 7, r=[wgb, hb[c]], w=[pab])
                    act(glog[:, j, :], pa[:, 0:18], AF.Exp, r=[pab], w=[glb], scale=-1.0)
                ts(glog[:], glog[:], 1.0, None, ALU.add, None, r=[glb], w=[glb])
                recip(glog[:], glog[:], r=[glb], w=[glb])
                attn_multi([dict(q=qts[h], k=kcT1, rows=DATR[h], vt_ap=vca[0:127, h // 3, 0:97], vtb=vcab, tiles=cmp_tiles,
                                 kparts=127, vcols=97, pvstride=97, evac=cmp_evac(h)) for h in range(6)], c)
                tt(imp[:], imp[:], CT["ns_A"][:, 4 * c:4 * c + 4, :].unsqueeze(2).to_broadcast([128, 4, 2, 32]), ALU.mult,
                   r=[impb] + cb, w=[impb])
                tt(imp[:], imp[:], CT["ns_B"][:, 4 * c:4 * c + 4, :].unsqueeze(2).to_broadcast([128, 4, 2, 32]), ALU.add,
                   r=[impb] + cb, w=[impb])
                for j in range(4):
                    for kv in range(2):
                        P.op("dve", lambda e, j=j, kv=kv: e.max(out=i8[:, j * 2 + kv, :], in_=imp[:, j, kv, :]), r=[impb], w=[i8b])
                tt(imp[:], imp[:], i8[:, :, 5:6].rearrange("p (j k) o -> p j k o", j=4).to_broadcast([128, 4, 2, 32]), ALU.is_ge,
                   r=[impb, i8b], w=[impb])
                ts(sst[:, :, 0, 64:96], imp[:, :, 0, :], BIG, -BIG, ALU.mult, ALU.add, r=[impb], w=[sstb])
                ts(sst[:, :, 1, 0:32], imp[:, :, 1, :], BIG, -BIG, ALU.mult, ALU.add, r=[impb], w=[sstb])
                for kv in range(2):
                    pa, pab = RG['aux'].next()
                    pav = pa[:].bitcast(BF16)
                    for j in range(4):
                        tr(pav[0:96, j * 128:(j + 1) * 128], sst[:, j, kv, :], ident_b[:], r=[sstb] + cb, w=[pab])
                    a0_, a1_ = (64, 96) if kv == 0 else (0, 64)
                    for g in range(3):
                        cp(qts[kv * 3 + g][0][a0_:a1_, :], pav[a0_:a1_, 0:512], r=[pab], w=[qts[kv * 3 + g][1]])
                tt(wts[:], wts[:], gv[:, :, :, 0], ALU.mult, r=[wtsb, glb], w=[wtsb])
                tt(acc[:], ost[:, :, :, 0:64], wts[:].unsqueeze(3).to_broadcast([128, 4, 6, 64]), ALU.mult, r=[ostb, wtsb], w=[accb])

            def A_units(c):
                qts = qsets[c % 2]
                ob, obb = obs[c % 2]
                units = []
                for br, (tiles_fn, RW, ks, slot0) in enumerate(((causal_tiles, SELR, ksel, 0), (win_tiles, DATR, [kwin1, kwin1], 2))):
                    units += attn_units([dict(q=qts[h], k=ks[h // 3], rows=RW[h], vt=vt, vtb=vtb, vslot=slot0 + h // 3, tiles=tiles_fn,
                                              evac=(lambda pv, pvb, h=h: cp(ost[:, :, h, 0:65], pv[:, 0:260].rearrange("p (j d) -> p j d", j=4),
                                                                            r=[pvb], w=[ostb]))) for h in range(6)], c)

                    def epi(br=br):
                        recip(wts[:], ost[:, :, :, 64], r=[ostb], w=[wtsb])
                        tt(wts[:], wts[:], gv[:, :, :, 1 + br], ALU.mult, r=[wtsb, glb], w=[wtsb])
                        tt(ost[:, :, :, 0:64], ost[:, :, :, 0:64], wts[:].unsqueeze(3).to_broadcast([128, 4, 6, 64]), ALU.mult,
                           r=[ostb, wtsb], w=[ostb])
                        if br == 0:
                            tt(acc[:], acc[:], ost[:, :, :, 0:64], ALU.add, r=[ostb, accb], w=[accb])
                        else:
                            tt(ob[:].rearrange("p j (h d) -> p j h d", h=6), acc[:], ost[:, :, :, 0:64], ALU.add, r=[ostb, accb], w=[obb])
                    units.append(epi)
                return units

            def D_units(c):
                ob, obb = obs[c % 2]
                return outproj_units(l, c, ob, obb, 3, wo, wob, oT, oTb)

            rings_alt(True)
            for u in P_units(0):
                u()
            SELN(0)
            for c in range(4):
                side = (D_units(c - 1) if c > 0 else []) + (P_units(c + 1) if c < 3 else [])
                run_interleaved(A_units(c), side)
                if c < 3:
                    SELN(c + 1)
            for u in D_units(3):
                u()
            rings_alt(False)

    def dilated(l):
        PH.append(('dil', dict(P.cnt)))
        pre_blk = [load_wcols(l, C_KC, 256), load_wcols(l, C_KC + 256, 128)]
        P.barrier()
        with ExitStack() as es:
            k01 = sb(es, "dk01", [128, S], BF16)
            kts = [k01, k01] + [sb(es, f"dk{h}", [128, S], BF16) for h in range(2, 6)]
            DROWS = [(0, 64), (64, 128), (0, 69), (0, 128), (0, 81), (0, 128)]
            vt, vtb = sb(es, "dv", [128, NT, 6, 65], BF16)
            memset(vt[:, :, :, 64:65], 1.0, w=[vtb])
            wq, wqb = sb(es, "wq", [128, 8, 384], BF16)
            load_w(wq[:], wqb, wview_in(l, C_QC, 384))
            wo, wob = load_wo(es, l, 640, 3)
            qsets = []
            for s_ in range(2):
                q01 = sb(es, f"dq01{s_}", [128, 512], BF16)
                qsets.append([q01, q01] + [sb(es, f"dq{s_}{h}", [128, 512], BF16) for h in range(2, 6)])
            for h, nm, dil in ((2, "d4", 4), (4, "d16", 16)):
                dma("sp", kts[h][0][64:65 + dil, :], CD["ka_" + nm], r=[], w=[kts[h][1]])
                dma("sp", kts[h + 1][0][0:64, :], CD["kb_" + nm], r=[], w=[kts[h + 1][1]])
                for s_ in range(2):
                    dma("sp", qsets[s_][h][0][64:65 + dil, :], CD["qa_" + nm], r=[], w=[qsets[s_][h][1]])
                    dma("sp", qsets[s_][h + 1][0][0:64, :], CD["qb_" + nm], r=[], w=[qsets[s_][h + 1][1]])
            for blk in range(2):
                wt, wtb = pre_blk[blk]
                pairs_run(l, [(wt, wtb, pp * 128, tc_ * 512, 512, HG["kn_c"], kts[blk * 4 + 2 * pp], kts[blk * 4 + 2 * pp + 1], tc_ * 512)
                              for pp in range(2 if blk == 0 else 1) for tc_ in range(4)])
            for blk in range(2):
                wt, wtb = load_wcols(l, C_VC + blk * 256, 256 if blk == 0 else 128)
                v_tm(wt, wtb, 0, 4 if blk == 0 else 2, vt, vtb, blk * 4)
            ost, ostb = sb(es, "ost", [128, 4, 6, 65], F32)
            dsum, dsb = sb(es, "dsum", [128, 4, 2], F32)
            obs = [sb(es, f"ob{i}", [128, 4, 384], BF16) for i in range(2)]
            oT, oTb = sb(es, "oT", [128, 3, 512], BF16)
            cfg = [(band_tiles(1, "m_le"), 64), (band_tiles(4, "m_le"), 69), (causal_tiles, 81)]

            def P_units(c):
                qts = qsets[c % 2]
                return pairs_units(l, [(wq, wqb, pp * 128, c * 512, 512, HG["qn_c"], qts[2 * pp], qts[2 * pp + 1], 0) for pp in range(3)])

            def A_units(c):
                qts = qsets[c % 2]
                return attn_units([dict(q=qts[h], k=kts[h], rows=DROWS[h], vt=vt, vtb=vtb, vslot=h, tiles=cfg[h // 2][0],
                                        evac=(lambda pv, pvb, h=h: cp(ost[:, :, h, :], pv[:, 0:260].rearrange("p (j d) -> p j d", j=4),
                                                                      r=[pvb], w=[ostb]))) for h in range(6)], c)

            def post(c):
                ob, obb = obs[c % 2]
                tt(dsum[:], ost[:, :, 0:2, 64], ost[:, :, 2:4, 64], ALU.add, r=[ostb], w=[dsb])
                tt(dsum[:], dsum[:], ost[:, :, 4:6, 64], ALU.add, r=[ostb, dsb], w=[dsb])
                recip(dsum[:], dsum[:], r=[dsb], w=[dsb])
                obv = ob[:].rearrange("p j (g i d) -> p j g i d", g=3, i=2)
                for g in range(3):
                    tt(obv[:, :, g, :, :], ost[:, :, 2 * g:2 * g + 2, 0:64], dsum[:].unsqueeze(3).to_broadcast([128, 4, 2, 64]),
                       ALU.mult, r=[ostb, dsb], w=[obb])

            def D_units(c):
                ob, obb = obs[c % 2]
                return outproj_units(l, c, ob, obb, 3, wo, wob, oT, oTb)

            rings_alt(True)
            for u in P_units(0):
                u()
            for c in range(4):
                side = (P_units(c + 1) if c < 3 else []) + (D_units(c - 1) if c > 0 else [])
                run_interleaved(A_units(c), side)
                post(c)
            for u in D_units(3):
                u()
            rings_alt(False)

    def ffn(l):
        PH.append(('ffn', dict(P.cnt)))
        wvg = W["w_gate"][l].rearrange("(kc p) n -> p kc n", p=128)
        wvu = W["w_up"][l].rearrange("(kc p) n -> p kc n", p=128)
        pre = []
        for wv in (wvg, wvu):
            t_, tb_ = wring.next()
            load_w(t_[:], tb_, wv[:, :, 0:256])
            pre.append((t_, tb_))
        P.barrier()
        with ExitStack() as es:
            wd, wdb = sb(es, "wd", [128, NFC, D], BF16)
            wdv = W["w_down"][l].rearrange("(fc p) n -> p fc n", p=128)
            wd_bufs = [P.buf(f"wd{i}") for i in range(NFC // 2)]
            cw = cw_all[:, l]
            cwb = cwallb
            hid, hidb = sb(es, "hid", [128, NFC, 512], BF16)
            carry, carb = sb(es, "carry", [128, NFC, 2], F32)
            memset(carry[:], 0.0, w=[carb])
            g_r = Ring([f32_items[i] + (P.buf(f"gbc{i}"),) for i in range(2)])
            pend = [None]
            a_r = Ring([f32_items[2 + i] for i in range(2)])
            hq_r = [sb(es, f"hq{i}", [128, 8, 512], BF16) for i in range(2)]
            fring = Ring([sb(es, f"fw{i}", [128, 8, 256], BF16) for i in range(4)] + list(wring.items))
            norm_T(l, 1, list(range(0, 4)), hq_r[0][0], lambda i: hq_r[0][1], lambda i: (i % 4) * 128)
            FH = 12
            hidh = [P.buf("hid_lo"), P.buf("hid_hi")]

            def down_part(q, f0, f1):
                for j in range(4):
                    i = 4 * q + j
                    for hf in range(2):
                        pa, pab = RG['aux'].next()
                        for fc in range(f0, f1):
                            mm(pa[:, :], hid[:, fc, j * 128:(j + 1) * 128], wd[:, fc, hf * 512:(hf + 1) * 512], fc == f0, fc == f1 - 1,
                               r=[hidh[fc // FH], wd_bufs[fc // 2]], w=[pab])
                        tt(xs[:, i, hf * 512:(hf + 1) * 512], xs[:, i, hf * 512:(hf + 1) * 512], pa[:, :], ALU.add,
                           r=[pab, xb[i]], w=[xb[i]])

            for q in range(4):
                hq, hqb = hq_r[q % 2]
                for blk in range(NFC // 2):
                    if q == 0 and blk == 0:
                        (wgt, wgtb), (wut, wutb) = pre
                    else:
                        wgt, wgtb = fring.next()
                        load_w(wgt[:], wgtb, wvg[:, :, blk * 256:(blk + 1) * 256])
                        wut, wutb = fring.next()
                        load_w(wut[:], wutb, wvu[:, :, blk * 256:(blk + 1) * 256])
                    if q == 0:
                        load_w(wd[:, 2 * blk:2 * blk + 2, :], wd_bufs[blk], wdv[:, 2 * blk:2 * blk + 2, :])
                    for s_ in range(2):
                        fc = blk * 2 + s_
                        pg, pgb = RG['proj'].next()
                        for kc in range(8):
                            mm(pg[:, :], wgt[:, kc, s_ * 128:(s_ + 1) * 128], hq[:, kc, :], kc == 0, kc == 7, r=[wgtb, hqb], w=[pgb])
                        pu, pub = RG['s'].next()
                        for kc in range(8):
                            mm(pu[:, :], wut[:, kc, s_ * 128:(s_ + 1) * 128], hq[:, kc, :], kc == 0, kc == 7, r=[wutb, hqb], w=[pub])
                        gb_, gbm, gbc = g_r.next()
                        cp(gb_[:, 0:2], carry[:, fc, :], r=[carb], w=[gbc])
                        act(gb_[:, 2:514], pg[:, :], AF.Copy, r=[pgb], w=[gbm])
                        a_, ab_ = a_r.next()
                        act(a_[:, 0:512], pg[:, :], AF.Identity, r=[pgb, cwb], w=[ab_], scale=cw[:, 2, fc:fc + 1], bias=cw[:, 3, fc:fc + 1])
                        cp(carry[:, fc, :], gb_[:, 512:514], r=[gbm], w=[carb])
                        stt(a_[:, 0:512], gb_[:, 1:513], cw[:, 1, fc:fc + 1], a_[:, 0:512], ALU.mult, ALU.add, r=[gbm, gbc, cwb, ab_], w=[ab_])
                        stt(a_[:, 0:512], gb_[:, 0:512], cw[:, 0, fc:fc + 1], a_[:, 0:512], ALU.mult, ALU.add, r=[gbm, gbc, cwb, ab_], w=[ab_])
                        act(a_[:, 0:512], a_[:, 0:512], AF.Silu, r=[ab_], w=[ab_])
                        if pend[0] is not None:
                            pfc, pa_, pab_, ppu, ppub = pend[0]
                            tt(hid[:, pfc, :], pa_[:, 0:512], ppu[:, :], ALU.mult, r=[pab_, ppub], w=[hidh[pfc // FH]])
                        pend[0] = (fc, a_, ab_, pu, pub)
                    if blk == FH // 2:
                        down_part(q, 0, FH)
                pfc, pa_, pab_, ppu, ppub = pend[0]
                tt(hid[:, pfc, :], pa_[:, 0:512], ppu[:, :], ALU.mult, r=[pab_, ppub], w=[hidh[pfc // FH]])
                pend[0] = None
                if q < 3:
                    hqn, hqnb = hq_r[(q + 1) % 2]
                    norm_T(l, 1, list(range(4 * q + 4, 4 * q + 8)), hqn, lambda i: hqnb, lambda i: (i % 4) * 128)
                down_part(q, FH, NFC)

    for b in range(NB):
        for i in range(NT):
            if b == 0 and i < 4:
                continue
            dma("sp", xs[:, i, :], x_d[b, i * 128:(i + 1) * 128, :], r=[], w=[xb[i]])
        for l in range(NL):
            PH.append(('norm', dict(P.cnt)))
            if not (b == 0 and l == 0):
                P.barrier()
            with ExitStack() as att:
                HT['t'], _ = sb(att, "hT", [128, 8, S], BF16)
                norm_T(l, 0, list(range(NT)), HT['t'], lambda i: hb[i // 4], lambda i: i * 128)
                moba(l)
                nsa(l)
                dilated(l)
            ffn(l)
        for i in range(NT):
            dma("sp", y_d[b, i * 128:(i + 1) * 128, :], xs[:, i, :], r=[xb[i]], w=[ybufs[i]])
    PH.append(('end', dict(P.cnt)))
    _CACHE['phases'] = PH
    P.final_wait("sp", ybufs)
    P.barrier()
    P.replay()
    root.close()
    return nc, consts


_CACHE = {}


def kernel(**inputs):
    n = 8
    x = np.ascontiguousarray(np.asarray(inputs["x"], dtype=np.float32))
    if "prog" not in _CACHE:
        _CACHE["prog"] = build(2, 2)
    nc, consts = _CACHE["prog"]
    in_maps = []
    for c in range(n):
        m = {"x": x[2 * c:2 * c + 2]}
        for k, v in inputs.items():
            if k != "x":
                m[k] = np.ascontiguousarray(np.asarray(v, dtype=np.float32))
        for k, v in consts.items():
            m["c_" + k] = v
        in_maps.append(m)
    res = run_bass_kernel_spmd(nc, in_maps, core_ids=list(range(n)))
    return np.concatenate([r["y"] for r in res.results], axis=0).astype(np.float32)
```
